# Optimizing a Trainium2 kernel written in Bass

```python
import jax, jax.numpy as jnp
import numpy as np

D_MODEL = 1024
BATCH = 8
SEQ = 4096
DEPTH = 4

HEAD_DIM = 64
BRANCH_WIDTH = D_MODEL
A_Q_HEADS = BRANCH_WIDTH // HEAD_DIM
A_KV_HEADS = 2
A_WINDOW = 128
B_Q_HEADS = BRANCH_WIDTH // HEAD_DIM
B_KV_HEADS = 4
B_PAIRS = ((128, 1), (512, 4), (2048, 16))
N_GROUPS = len(B_PAIRS)
BLOCK = 128
ROT_DIM = HEAD_DIM // 4
ROPE_THETA = 500000.0
EPS = 1e-6
N_MIXERS = 2
N_A = (DEPTH + 1) // 2
N_B = DEPTH // 2
A_COLS = (A_Q_HEADS + 2 * A_KV_HEADS) * HEAD_DIM + BRANCH_WIDTH
B_GROUP_COLS = (B_Q_HEADS + 2 * B_KV_HEADS) * HEAD_DIM
B_COLS = N_GROUPS * B_GROUP_COLS + BRANCH_WIDTH
SCALE = HEAD_DIM ** -0.5

kernel_name = "hybrid_swa_sink_dilated_gated_trunk"


def rmsnorm(x, g):
    xf = x.astype(jnp.float32)
    y = xf * jax.lax.rsqrt(jnp.mean(xf * xf, axis=-1, keepdims=True) + EPS)
    return (y * g.astype(jnp.float32)).astype(x.dtype)


def rope_tables(seq):
    pos = jnp.arange(seq, dtype=jnp.float32)
    inv = ROPE_THETA ** (-jnp.arange(0, ROT_DIM, 2, dtype=jnp.float32) / ROT_DIM)
    ang = pos[:, None] * inv[None, :]
    return jnp.cos(ang), jnp.sin(ang)


def partial_rope(t, cos, sin):
    tf = t.astype(jnp.float32)
    half = ROT_DIM // 2
    t1, t2, rest = tf[..., :half], tf[..., half:ROT_DIM], tf[..., ROT_DIM:]
    c = cos[None, :, None, :]
    s = sin[None, :, None, :]
    out = jnp.concatenate([t1 * c - t2 * s, t2 * c + t1 * s, rest], axis=-1)
    return out.astype(t.dtype)


def banded_attention(q, k, v, max_dist, sink=None):
    n, L, hq, d = q.shape
    hkv = k.shape[2]
    rep = hq // hkv
    nb = -(-L // BLOCK)
    lp = nb * BLOCK
    pad = ((0, 0), (0, lp - L), (0, 0), (0, 0))
    q, k, v = jnp.pad(q, pad), jnp.pad(k, pad), jnp.pad(v, pad)
    qb = q.reshape(n, nb, BLOCK, hkv, rep, d)

    def with_prev(t):
        tb = t.reshape(n, nb, BLOCK, hkv, d)
        prev = jnp.pad(tb, ((0, 0), (1, 0), (0, 0), (0, 0), (0, 0)))[:, :-1]
        return jnp.concatenate([prev, tb], axis=2)

    kk, vv = with_prev(k), with_prev(v)
    s = jnp.einsum('nbqgrd,nbkgd->nbgrqk', qb, kk,
                   preferred_element_type=jnp.float32) * SCALE
    i = jnp.arange(BLOCK)[:, None]
    j = jnp.arange(2 * BLOCK)[None, :]
    dist = BLOCK + i - j
    blk = jnp.arange(nb)[:, None, None]
    valid = (dist >= 0) & (dist <= max_dist) & ((blk > 0) | (j >= BLOCK))
    s = jnp.where(valid[None, :, None, None], s, -jnp.inf)
    m = jnp.max(s, axis=-1)
    if sink is not None:
        sk = sink.astype(jnp.float32).reshape(hkv, rep)[None, None, :, :, None]
        m = jnp.maximum(m, sk)
    p = jnp.exp(s - m[..., None])
    l = jnp.sum(p, axis=-1)
    if sink is not None:
        l = l + jnp.exp(sk - m)
    o = jnp.einsum('nbgrqk,nbkgd->nbgrqd', p, vv.astype(jnp.float32)) / l[..., None]
    o = o.transpose(0, 1, 4, 2, 3, 5).reshape(n, lp, hq, d)[:, :L]
    lse = (m + jnp.log(l)).transpose(0, 1, 4, 2, 3).reshape(n, lp, hq)[:, :L]
    return o.astype(q.dtype), lse


def dilated_group(q, k, v, window, dilation):
    b, s, hq, d = q.shape
    sub = s // dilation

    def fold(t):
        return t.reshape(b, sub, dilation, t.shape[2], d).swapaxes(1, 2).reshape(b * dilation, sub, t.shape[2], d)

    o, lse = banded_attention(fold(q), fold(k), fold(v), window // dilation)
    o = o.reshape(b, dilation, sub, hq, d).swapaxes(1, 2).reshape(b, s, hq, d)
    lse = lse.reshape(b, dilation, sub, hq).swapaxes(1, 2).reshape(b, s, hq)
    return o, lse


def mixer_a(h, w_in, q_gain, k_gain, sinks, cos, sin):
    b, s, _ = h.shape
    z = h @ w_in
    nq, nk = A_Q_HEADS * HEAD_DIM, A_KV_HEADS * HEAD_DIM
    q, k, v, gate = jnp.split(z, [nq, nq + nk, nq + 2 * nk], axis=-1)
    q = partial_rope(rmsnorm(q.reshape(b, s, A_Q_HEADS, HEAD_DIM), q_gain), cos, sin)
    k = partial_rope(rmsnorm(k.reshape(b, s, A_KV_HEADS, HEAD_DIM), k_gain), cos, sin)
    v = v.reshape(b, s, A_KV_HEADS, HEAD_DIM)
    o, _ = banded_attention(q, k, v, A_WINDOW - 1, sink=sinks)
    return o.reshape(b, s, BRANCH_WIDTH) * jax.nn.silu(gate)


def mixer_b(h, w_in, q_gain, k_gain, cos, sin):
    b, s, _ = h.shape
    z = h @ w_in
    heads = z[..., :N_GROUPS * B_GROUP_COLS].reshape(b, s, N_GROUPS, B_GROUP_COLS)
    gate = z[..., N_GROUPS * B_GROUP_COLS:]
    nq, nk = B_Q_HEADS * HEAD_DIM, B_KV_HEADS * HEAD_DIM
    outs, lses = [], []
    for g, (window, dilation) in enumerate(B_PAIRS):
        q, k, v = jnp.split(heads[:, :, g], [nq, nq + nk], axis=-1)
        q = partial_rope(rmsnorm(q.reshape(b, s, B_Q_HEADS, HEAD_DIM), q_gain[g]), cos, sin)
        k = partial_rope(rmsnorm(k.reshape(b, s, B_KV_HEADS, HEAD_DIM), k_gain[g]), cos, sin)
        v = v.reshape(b, s, B_KV_HEADS, HEAD_DIM)
        o, lse = dilated_group(q, k, v, window, dilation)
        outs.append(o)
        lses.append(lse)
    wts = jax.nn.softmax(jnp.stack(lses), axis=0)
    o = jnp.einsum('gbsh,gbshd->bshd', wts, jnp.stack(outs).astype(jnp.float32))
    return o.reshape(b, s, BRANCH_WIDTH).astype(h.dtype) * jax.nn.silu(gate)


def setup_inputs(seed: int = 0) -> dict:
    key = jax.random.key(seed)
    ks = jax.random.split(key, 12)
    f32 = jnp.float32
    x = jax.random.normal(ks[0], (BATCH, SEQ, D_MODEL), f32)
    norm_a = 1.0 + 0.02 * jax.random.normal(ks[1], (N_A, D_MODEL), f32)
    w_in_a = jax.random.normal(ks[2], (N_A, D_MODEL, A_COLS), f32) * D_MODEL ** -0.5
    q_gain_a = 1.0 + 0.02 * jax.random.normal(ks[3], (N_A, HEAD_DIM), f32)
    k_gain_a = 1.0 + 0.02 * jax.random.normal(ks[4], (N_A, HEAD_DIM), f32)
    sinks_a = jax.random.normal(ks[5], (N_A, A_Q_HEADS), f32)
    w_out_a = jax.random.normal(ks[6], (N_A, BRANCH_WIDTH, D_MODEL), f32) * BRANCH_WIDTH ** -0.5
    norm_b = 1.0 + 0.02 * jax.random.normal(ks[7], (N_B, D_MODEL), f32)
    w_in_b = jax.random.normal(ks[8], (N_B, D_MODEL, B_COLS), f32) * D_MODEL ** -0.5
    q_gain_b = 1.0 + 0.02 * jax.random.normal(ks[9], (N_B, N_GROUPS, HEAD_DIM), f32)
    k_gain_b = 1.0 + 0.02 * jax.random.normal(ks[10], (N_B, N_GROUPS, HEAD_DIM), f32)
    w_out_b = jax.random.normal(ks[11], (N_B, BRANCH_WIDTH, D_MODEL), f32) * BRANCH_WIDTH ** -0.5
    return {"x": x, "norm_a": norm_a, "w_in_a": w_in_a, "q_gain_a": q_gain_a,
            "k_gain_a": k_gain_a, "sinks_a": sinks_a, "w_out_a": w_out_a,
            "norm_b": norm_b, "w_in_b": w_in_b, "q_gain_b": q_gain_b,
            "k_gain_b": k_gain_b, "w_out_b": w_out_b}


def reference(x, norm_a, w_in_a, q_gain_a, k_gain_a, sinks_a, w_out_a,
              norm_b, w_in_b, q_gain_b, k_gain_b, w_out_b):
    cos, sin = rope_tables(x.shape[1])
    for layer in range(DEPTH):
        idx = layer // N_MIXERS
        if layer % N_MIXERS == 0:
            h = rmsnorm(x, norm_a[idx])
            y = mixer_a(h, w_in_a[idx], q_gain_a[idx], k_gain_a[idx], sinks_a[idx], cos, sin)
            x = x + y @ w_out_a[idx]
        else:
            h = rmsnorm(x, norm_b[idx])
            y = mixer_b(h, w_in_b[idx], q_gain_b[idx], k_gain_b[idx], cos, sin)
            x = x + y @ w_out_b[idx]
    return x
```

```python
import numpy as np
from contextlib import ExitStack
import concourse.bass as bass
import concourse.mybir as mybir
from concourse.bass_utils import run_bass_kernel_spmd

F32 = mybir.dt.float32
BF16 = mybir.dt.bfloat16
AF = mybir.ActivationFunctionType
ALU = mybir.AluOpType
AX = mybir.AxisListType

S = 4096
D = 1024
NBLK = 32
EPS = 1e-6
ROPE_THETA = 500000.0
B_PAIRS = ((128, 1), (512, 4), (2048, 16))


class _Op:
    __slots__ = ("eng", "fn", "deps", "dma", "signal", "seq", "sem")

    def __init__(self, eng, fn, deps, dma):
        self.eng, self.fn, self.deps, self.dma = eng, fn, deps, dma
        self.signal = False
        self.seq = 0
        self.sem = None


class Prog:
    def __init__(self, nc):
        self.nc = nc
        self.ops = []
        self.last_w = {}
        self.readers = {}

    def add(self, eng, fn, r=(), w=(), dma=None):
        i = len(self.ops)
        deps = set()
        for k in r:
            j = self.last_w.get(k)
            if j is not None:
                deps.add(j)
        for k in w:
            j = self.last_w.get(k)
            if j is not None:
                deps.add(j)
            for x in self.readers.get(k, ()):
                deps.add(x)
        self.ops.append(_Op(eng, fn, deps, dma))
        for k in r:
            self.readers.setdefault(k, []).append(i)
        for k in w:
            self.last_w[k] = i
            self.readers[k] = []
        return i

    def emit(self, final_dma_sems=(), max_ops=None):
        nc = self.nc
        ops = self.ops if max_ops is None else self.ops[:max_ops]

        def skip(op, dop):
            return op.eng == "pe" and dop.eng == "pe" and dop.dma is None and op.dma is None

        for op in ops:
            for d in op.deps:
                if not skip(op, ops[d]):
                    ops[d].signal = True
        cnt = {}
        dma_names = []
        for op in ops:
            if op.dma is not None:
                if op.dma not in cnt:
                    dma_names.append(op.dma)
                cnt[op.dma] = cnt.get(op.dma, 0) + 16
                op.seq = cnt[op.dma]
                op.sem = op.dma
            elif op.signal:
                cnt[op.eng] = cnt.get(op.eng, 0) + 1
                op.seq = cnt[op.eng]
                op.sem = op.eng
        with ExitStack() as es:
            sems = {}
            for name in ["pe", "act", "dve", "pool"] + dma_names:
                sems[name] = es.enter_context(nc.semaphore("s_" + name))
            block = es.enter_context(nc.Block())

            def run(engname, eng):
                waited = {}
                for op in ops:
                    if op.eng != engname:
                        continue
                    needs = {}
                    for d in op.deps:
                        dop = ops[d]
                        if skip(op, dop):
                            continue
                        if needs.get(dop.sem, 0) < dop.seq:
                            needs[dop.sem] = dop.seq
                    for sname, val in needs.items():
                        if waited.get(sname, 0) < val:
                            eng.wait_ge(sems[sname], val)
                            waited[sname] = val
                    ins = op.fn(eng)
                    if op.dma is not None:
                        ins.then_inc(sems[op.dma], 16)
                    elif op.signal:
                        ins.then_inc(sems[engname], 1)
                if engname == "sp":
                    for name in final_dma_sems:
                        if name in cnt:
                            eng.wait_ge(sems[name], cnt[name])

            @block.tensor
            def _(e):
                run("pe", e)

            @block.scalar
            def _(e):
                run("act", e)

            @block.vector
            def _(e):
                run("dve", e)

            @block.gpsimd
            def _(e):
                run("pool", e)

            @block.sync
            def _(e):
                run("sp", e)


class Pass:
    def __init__(self, kind, layer, idx, g=0):
        self.kind, self.layer, self.idx, self.g = kind, layer, idx, g
        if kind == "A":
            self.delta, self.nkv, self.ncols = 1, 2, 2304
            self.slabs = [(0, 512), (512, 512), (1024, 256), (1280, 512), (1792, 512)]
            self.wname, self.wcol0 = "w_in_a", 0
        elif kind == "G":
            self.delta, self.nkv, self.ncols = B_PAIRS[g][1], 4, 1536
            self.slabs = [(0, 512), (512, 512), (1024, 512)]
            self.wname, self.wcol0 = "w_in_b", 1536 * g
        else:
            self.delta, self.nkv, self.ncols = 1, 0, 1024
            self.slabs = [(0, 512), (512, 512)]
            self.wname, self.wcol0 = "w_in_b", 4608
        self.nh = 16 + self.nkv
        self.has_out = kind in ("A", "C")
        self.bpc = NBLK // self.delta

    def rows(self, t, fb):
        if self.delta == 1:
            return t[fb * 128:(fb + 1) * 128, :]
        r, nb = fb // self.bpc, fb % self.bpc
        return t.rearrange("(j r) e -> r j e", r=self.delta)[r, nb * 128:(nb + 1) * 128, :]

    def nat_blocks(self, fb):
        if self.delta == 1:
            return [fb]
        nb = fb % self.bpc
        return list(range(nb * self.delta, (nb + 1) * self.delta))

    def cls(self, fb):
        return fb // self.bpc


def make_passes(n_layers):
    ps = []
    for layer in range(n_layers):
        idx = layer // 2
        if layer % 2 == 0:
            ps.append(Pass("A", layer, idx))
        else:
            for g in range(3):
                ps.append(Pass("G", layer, idx, g))
            ps.append(Pass("C", layer, idx))
    return ps


class _Rec:
    def __init__(self):
        self.segs = {}
        self.order = []
        self.cur = None
        self.seg("pre")

    def add(self, *a, **k):
        self.segs[self.cur].append((a, k))

    def seg(self, name):
        self.cur = name
        if name not in self.segs:
            self.segs[name] = []
            self.order.append(name)

    def lst(self):
        return [self.segs[n] for n in self.order if self.segs[n]]


def build(n_layers=4, max_ops=None, marks=None):
    nc = bass.Bass("TRN2", target_bir_lowering=False)
    dr = {}

    def din(name, shape):
        dr[name] = nc.dram_tensor(name, list(shape), F32, kind="ExternalInput").ap()

    din("x", (S, D))
    din("w_in_a", (2, D, 2304))
    din("w_out_a", (2, D, D))
    din("w_in_b", (2, D, 5632))
    din("w_out_b", (2, D, D))
    din("normw_a", (2, 128, 8))
    din("normw_b", (2, 128, 8))
    din("gain_a", (2, 128, 18 * 64))
    din("gain_b", (2, 3, 128, 20 * 64))
    din("sinks", (2, 128, 16))
    din("rope", (3, 2, 128, NBLK * 8))
    din("consts", (4, 128, 128))
    out = nc.dram_tensor("out", [S, D], F32, kind="ExternalOutput").ap()
    num = [nc.dram_tensor("num%d" % g, [S, 1040], F32, kind="Internal").ap() for g in range(3)]

    P = Prog(nc)
    es = ExitStack()
    lp = nc.allow_low_precision("bf16 matmul operands, fp32 accumulation (problem tolerance)")
    lp.__enter__()

    def sb(name, shape, dt=F32):
        return es.enter_context(nc.sbuf_tensor(name, list(shape), dt))

    def ps(name, shape, dt=F32):
        return es.enter_context(nc.psum_tensor(name, list(shape), dt))

    ident = sb("ident", [128, 128], BF16)
    mask_cur = sb("mask_cur", [128, 128], BF16)
    mask_pa = sb("mask_pa", [128, 128], BF16)
    mask_pb = sb("mask_pb", [128, 128], BF16)
    negh = sb("negh", [128, 32])
    win = [sb("win%d" % i, [128, 8, 2304], BF16) for i in range(2)]
    wout = [sb("wout%d" % i, [128, 8, 1024], BF16) for i in range(2)]
    wstage = [sb("wstage%d" % i, [128, 8, 128]) for i in range(2)]
    normw = [sb("normw%d" % i, [128, 8]) for i in range(2)]
    gain = [sb("gain%d" % i, [128, 20, 64]) for i in range(2)]
    ropec = [sb("ropec%d" % i, [128, NBLK, 8]) for i in range(2)]
    ropes = [sb("ropes%d" % i, [128, NBLK, 8]) for i in range(2)]
    sinkraw = [sb("sinkraw%d" % i, [128, 16]) for i in range(2)]
    esink = [sb("esink%d" % i, [128, 16]) for i in range(2)]
    xin = [sb("xin%d" % i, [128, 1040]) for i in range(4)]
    xb = sb("xb", [128, D], BF16)
    ss = sb("ss", [128, 1])
    rt = sb("rt", [128, 1])
    rstd = [sb("rstd%d" % i, [128, 1]) for i in range(2)]
    rstdh = [sb("rstdh%d" % i, [128, 1]) for i in range(4)]
    xT = [sb("xT%d" % i, [128, 8, 128], BF16) for i in range(2)]
    big = sb("big", [128, 3200])
    zqk = big[:, 0:1280].rearrange("p (h d) -> p h d", d=64)
    sq = big[:, 1280:2560].rearrange("p (h d) -> p h d", d=64)
    numin = big[:, 0:3120].rearrange("p (g h e) -> p g h e", g=3, e=65)
    BIGK = [("zqk", 0), ("zqk", 1), ("zqk", 2), ("sq", 0), ("sq", 1), ("sq", 2)]
    ssh = sb("ssh", [128, 20])
    rht = sb("rht", [128, 20])
    rh = sb("rh", [128, 20])
    ra = sb("ra", [128, 20, 8])
    rb = sb("rb", [128, 20, 8])
    rc = sb("rc", [128, 20, 8])
    rd = sb("rd", [128, 20, 8])
    qbuf = sb("qbuf", [128, 20, 64], BF16)
    kdup = sb("kdup", [128, 4, 2, 64], BF16)
    QT = [sb("QT%d" % i, [128, 8, 128], BF16) for i in range(3)]
    KT2 = [sb("KT2_%d" % i, [128, 4, 128], BF16) for i in range(4)]
    Vp = [sb("Vp%d" % i, [128, 4, 66], BF16) for i in range(4)]
    th = [sb("th%d" % i, [128, 16, 64]) for i in range(3)]
    Pb = [sb("Pb%d" % i, [128, 2, 256], BF16) for i in range(2)]
    lsum = sb("lsum", [128, 16])
    rl = sb("rl", [128, 16])
    fsc = sb("fsc", [128, 16])
    yb = sb("yb", [128, D], BF16)
    yT = sb("yT", [128, 8, 128], BF16)
    tp = ps("tp", [128, 8, 128], BF16)
    zp = [ps("zp%d" % i, [128, 512]) for i in range(2)]
    sp = [ps("sp%d" % i, [128, 2, 256]) for i in range(2)]
    Op = ps("Op", [128, 3, 512])

    def Ohead(h):
        return Op[:, h // 6, (h % 6) * 65:(h % 6) * 65 + 65]

    def Obank(b):
        nhb = 6 if b < 2 else 4
        return Op[:, b, 0:nhb * 65].rearrange("p (h e) -> p h e", e=65), b * 6, nhb

    cstage = xin[3][:, 0:512].rearrange("p (c n) -> p c n", n=128)
    P.add("sp", lambda e: e.dma_start(out=cstage, in_=dr["consts"].rearrange("c p n -> p c n")),
          w=[("xin", 3)], dma="d_const")
    for i, t in enumerate([ident, mask_cur, mask_pa, mask_pb]):
        P.add("dve", lambda e, i=i, t=t: e.tensor_copy(out=t[:], in_=cstage[:, i, :]),
              r=[("xin", 3)], w=["const%d" % i])
    P.add("pool", lambda e: e.memset(negh[:], -0.5), w=["negh"])
    for s_ in range(4):
        P.add("pool", lambda e, s_=s_: e.memset(Vp[s_][:, :, 64:66], 1.0), w=[("Vp1", s_)])

    passes = make_passes(n_layers)

    def wkeys(name, slot, c0, wd, c):
        return [(name, slot, u, c) for u in range(c0 // 128, (c0 + wd) // 128)]

    def load_steps(p, slot):
        steps = []
        L = p.idx
        wsrc = dr[p.wname][L].rearrange("(c q) n -> q c n", q=128)
        nw = dr["normw_a" if p.kind == "A" else "normw_b"][L]

        def small():
            P.add("sp", lambda e: e.dma_start(out=normw[slot][:], in_=nw), w=[("normw", slot)],
                  dma="d_normw%d" % slot)
            if p.kind != "C":
                gsrc = dr["gain_a"][L] if p.kind == "A" else dr["gain_b"][L, p.g]
                P.add("sp", lambda e: e.dma_start(
                    out=gain[slot][:, 0:p.nh, :].rearrange("p h d -> p (h d)"), in_=gsrc),
                    w=[("gain", slot)], dma="d_gain%d" % slot)
                oi = {1: 0, 4: 1, 16: 2}[p.delta]
                P.add("sp", lambda e: e.dma_start(
                    out=ropec[slot][:].rearrange("p b f -> p (b f)"), in_=dr["rope"][oi, 0]),
                    w=[("ropec", slot)], dma="d_ropec%d" % slot)
                P.add("sp", lambda e: e.dma_start(
                    out=ropes[slot][:].rearrange("p b f -> p (b f)"), in_=dr["rope"][oi, 1]),
                    w=[("ropes", slot)], dma="d_ropes%d" % slot)
            if p.kind == "A":
                P.add("sp", lambda e: e.dma_start(out=sinkraw[slot][:], in_=dr["sinks"][L]),
                      w=[("sinkraw", slot)], dma="d_sink%d" % slot)
                P.add("act", lambda e: e.activation(out=esink[slot][:], in_=sinkraw[slot][:],
                                                    func=AF.Exp),
                      r=[("sinkraw", slot)], w=[("esink", slot)])

        steps.append(small)
        slabs = [("in", c0, 128) for c0 in range(0, p.ncols, 128)]
        if p.has_out:
            osrc = dr["w_out_a" if p.kind == "A" else "w_out_b"][L].rearrange(
                "(c q) n -> q c n", q=128)
            slabs += [("out", c0, 128) for c0 in range(0, 1024, 128)]

        def dma_slab(j):
            kind, c0, wd = slabs[j]
            st = j % 2
            if kind == "in":
                P.add("sp", lambda e: e.dma_start(
                    out=wstage[st][:, :, 0:wd], in_=wsrc[:, :, p.wcol0 + c0:p.wcol0 + c0 + wd]),
                    w=[("wstage", st)], dma="d_wst%d" % st)
            else:
                P.add("sp", lambda e: e.dma_start(out=wstage[st][:], in_=osrc[:, :, c0:c0 + 128]),
                      w=[("wstage", st)], dma="d_wst%d" % st)

        def cast_slab(j):
            kind, c0, wd = slabs[j]
            st = j % 2
            for c in range(8):
                if kind == "in":
                    P.add("dve", lambda e, c=c: e.tensor_scalar(
                        out=win[slot][:, c, c0:c0 + wd], in0=wstage[st][:, c, 0:wd],
                        scalar1=normw[slot][:, c:c + 1], scalar2=None, op0=ALU.mult),
                        r=[("wstage", st), ("normw", slot)], w=wkeys("win", slot, c0, wd, c))
                else:
                    P.add("act", lambda e, c=c: e.copy(
                        out=wout[slot][:, c, c0:c0 + 128], in_=wstage[st][:, c, :]),
                        r=[("wstage", st)], w=wkeys("wout", slot, c0, 128, c))

        def mk(j):
            def f():
                if j < len(slabs):
                    dma_slab(j)
                if j >= 1:
                    cast_slab(j - 1)
            return f

        for j in range(len(slabs) + 1):
            steps.append(mk(j))
        return steps

    def x_load(p, pi, fb, gi):
        slot = gi % 4
        src = dr["x"] if pi == 0 else out
        P.add("sp", lambda e: e.dma_start(out=xin[slot][:, 0:D], in_=p.rows(src, fb)),
              r=[("xd", b) for b in p.nat_blocks(fb)], w=[("xin", slot)], dma="d_xin%d" % slot)

    def num_load(p, pi, fb):
        if p.kind == "C":
            for g in range(3):
                dlt = B_PAIRS[g][1]
                P.add("sp", lambda e, g=g: e.dma_start(
                    out=numin[:, g].rearrange("p h e -> p (h e)"),
                    in_=num[g][fb * 128:(fb + 1) * 128, :]),
                    r=[("num", g, fb, r_) for r_ in range(dlt)], w=[("numin", g)] + BIGK,
                    dma="d_numin%d" % g)

    def stage_pre(p, pi, fb, gi):
        Q = _Rec()
        slot = gi % 4
        xs = xin[slot][:, 0:D]
        r2, h3, t2 = gi % 2, gi % 4, gi % 2
        Q.seg("preA")
        Q.add("act", lambda e: e.activation(out=xb[:], in_=xs, func=AF.Square, accum_out=ss[:]),
              r=[("xin", slot)], w=["xb", "ss"])
        Q.add("act", lambda e: e.copy(out=xb[:], in_=xs), r=[("xin", slot)], w=["xb"])
        Q.add("pool", lambda e: e.tensor_scalar(out=rt[:], in0=ss[:], scalar1=1.0 / D, scalar2=EPS,
                                                op0=ALU.mult, op1=ALU.add), r=["ss"], w=["rt"])
        Q.add("pool", lambda e: e.tensor_tensor(out=rstd[r2][:], in0=rt[:], in1=negh[:, 0:1], op=ALU.pow),
              r=["rt", "negh"], w=[("rstd", r2)])
        if p.kind != "G":
            Q.add("pool", lambda e: e.tensor_scalar(out=rstdh[h3][:], in0=rstd[r2][:], scalar1=0.5,
                                                    scalar2=1.0, op0=ALU.mult, op1=ALU.mult),
                  r=[("rstd", r2)], w=[("rstdh", h3)])
        Q.seg("preT")
        for c in range(8):
            Q.add("pe", lambda e, c=c: e.transpose(out=tp[:, c, :], in_=xb[:, c * 128:(c + 1) * 128],
                                                   identity=ident[:]),
                  r=["xb", "const0"], w=["tp"])
        Q.add("act", lambda e: e.copy(out=xT[t2][:], in_=tp[:]), r=["tp"], w=[("xT", t2)])
        return Q

    def stage1(p, pi, fb, ws, gi):
        Q = _Rec()
        bp = gi % 3
        k3 = gi % 4
        r2, h3, xts = gi % 2, gi % 4, gi % 2
        nh, nkv = p.nh, p.nkv

        def slab_mm(si, c0, wd):
            z = zp[si % 2]
            for c in range(8):
                Q.add("pe", lambda e, c=c: e.matmul(z[:, 0:wd], lhsT=xT[xts][:, c, :],
                                                    rhs=win[ws][:, c, c0:c0 + wd],
                                                    start=(c == 0), stop=(c == 7)),
                      r=[("xT", xts)] + wkeys("win", ws, c0, wd, c), w=[("zp", si % 2)])
            return z

        zflat = big[:, 0:1280]
        thf = th[bp][:].rearrange("p h d -> p (h d)")
        if p.kind in ("A", "G"):
            for si in range(2):
                c0, wd = p.slabs[si]
                Q.seg("slab%d" % si)
                z = slab_mm(si, c0, wd)
                Q.add("dve", lambda e, z=z, c0=c0, wd=wd: e.tensor_scalar(
                    out=zflat[:, c0:c0 + wd], in0=z[:, 0:wd], scalar1=rstd[r2][:], scalar2=None,
                    op0=ALU.mult), r=[("zp", si % 2), ("rstd", r2)], w=[("zqk", si)])
            c0, wd = p.slabs[2]
            Q.seg("slab2")
            z = slab_mm(2, c0, wd)
            kw = 64 * nkv
            Q.add("dve", lambda e, z=z: e.tensor_scalar(
                out=zflat[:, 1024:1024 + kw], in0=z[:, 0:kw], scalar1=rstd[r2][:], scalar2=None,
                op0=ALU.mult), r=[("zp", 0), ("rstd", r2)], w=[("zqk", 2)])
            Q.add("dve", lambda e, z=z: e.tensor_scalar(
                out=Vp[k3][:, 0:nkv, 0:64], in0=z[:, kw:2 * kw].rearrange("p (g d) -> p g d", d=64),
                scalar1=rstd[r2][:], scalar2=None, op0=ALU.mult), r=[("zp", 0), ("rstd", r2)],
                w=[("Vp", k3)])
            gslabs = [(3, p.slabs[3]), (4, p.slabs[4])] if p.kind == "A" else []
            gbase = 1280
        else:
            gslabs = [(0, p.slabs[0]), (1, p.slabs[1])]
            gbase = 0
        for gi, (si, (c0, wd)) in enumerate(gslabs):
            Q.seg("gate%d" % gi)
            z = slab_mm(si, c0, wd)
            g0 = c0 - gbase
            Q.add("act", lambda e, z=z, g0=g0, wd=wd: e.activation(
                out=thf[:, g0:g0 + wd], in_=z[:, 0:wd], func=AF.Tanh, scale=rstdh[h3][:]),
                r=[("zp", si % 2), ("rstdh", h3)], w=[("th", bp, g0)])
            Q.add("dve", lambda e, z=z, g0=g0, wd=wd: e.scalar_tensor_tensor(
                out=thf[:, g0:g0 + wd], in0=thf[:, g0:g0 + wd], scalar=1.0, in1=z[:, 0:wd],
                op0=ALU.add, op1=ALU.mult), r=[("zp", si % 2), ("th", bp, g0)], w=[("th", bp, g0)])
        if p.kind == "C":
            return Q

        pieces = [(0, 0, 8), (1, 8, 16), (2, 16, 16 + nkv)]
        for pc, h0, h1 in pieces:
            n = h1 - h0
            ZK = [("zqk", pc)]
            Q.seg("c%d1" % pc)
            Q.add("act", lambda e, h0=h0, h1=h1: e.activation(out=sq[:, h0:h1, :], in_=zqk[:, h0:h1, :],
                                                              func=AF.Square), r=ZK, w=[("sq", pc)])
            Q.add("dve", lambda e, h0=h0, h1=h1: e.tensor_reduce(out=ssh[:, h0:h1], in_=sq[:, h0:h1, :],
                                                                 axis=AX.X, op=ALU.add),
                  r=[("sq", pc)], w=[("ssh", pc)])
            Q.seg("c%d2" % pc)
            Q.add("pool", lambda e, h0=h0, h1=h1: e.tensor_scalar(
                out=rht[:, h0:h1], in0=ssh[:, h0:h1], scalar1=1.0 / 64, scalar2=EPS, op0=ALU.mult,
                op1=ALU.add), r=[("ssh", pc)], w=[("rht", pc)])
            Q.add("pool", lambda e, h0=h0, h1=h1: e.tensor_tensor(
                out=rh[:, h0:h1], in0=rht[:, h0:h1], in1=negh[:, h0:h1], op=ALU.pow),
                r=[("rht", pc), "negh"], w=[("rh", pc)])
            Q.seg("c%d3" % pc)
            Q.add("dve", lambda e, h0=h0, h1=h1, n=n: e.tensor_tensor(
                out=zqk[:, h0:h1, :], in0=zqk[:, h0:h1, :],
                in1=rh[:, h0:h1].unsqueeze(2).to_broadcast([128, n, 64]), op=ALU.mult),
                r=ZK + [("rh", pc)], w=ZK)
            Q.add("dve", lambda e, h0=h0, h1=h1: e.tensor_tensor(
                out=zqk[:, h0:h1, 0:16], in0=zqk[:, h0:h1, 0:16], in1=gain[ws][:, h0:h1, 0:16],
                op=ALU.mult), r=ZK + [("gain", ws)], w=ZK)
            Q.add("dve", lambda e, h0=h0, h1=h1: e.tensor_tensor(
                out=qbuf[:, h0:h1, 16:64], in0=zqk[:, h0:h1, 16:64], in1=gain[ws][:, h0:h1, 16:64],
                op=ALU.mult), r=ZK + [("gain", ws)], w=[("qbuf", pc, 2)])
            Q.seg("c%d4" % pc)
            cb = ropec[ws][:, fb, :].unsqueeze(1).to_broadcast([128, n, 8])
            sbb = ropes[ws][:, fb, :].unsqueeze(1).to_broadcast([128, n, 8])
            t1 = zqk[:, h0:h1, 0:8]
            t2 = zqk[:, h0:h1, 8:16]
            RR = ZK + [("ropec", ws), ("ropes", ws)]
            for nm, tt, a_, b_ in (("ra", ra, t1, cb), ("rb", rb, t2, sbb), ("rc", rc, t2, cb),
                                   ("rd", rd, t1, sbb)):
                Q.add("pool", lambda e, tt=tt, a_=a_, b_=b_, h0=h0, h1=h1: e.tensor_tensor(
                    out=tt[:, h0:h1, :], in0=a_, in1=b_, op=ALU.mult), r=RR, w=[(nm, pc)])
            Q.add("pool", lambda e, h0=h0, h1=h1: e.tensor_tensor(
                out=qbuf[:, h0:h1, 0:8], in0=ra[:, h0:h1, :], in1=rb[:, h0:h1, :], op=ALU.subtract),
                r=[("ra", pc), ("rb", pc)], w=[("qbuf", pc, 0)])
            Q.add("pool", lambda e, h0=h0, h1=h1: e.tensor_tensor(
                out=qbuf[:, h0:h1, 8:16], in0=rc[:, h0:h1, :], in1=rd[:, h0:h1, :], op=ALU.add),
                r=[("rc", pc), ("rd", pc)], w=[("qbuf", pc, 1)])
            QB = [("qbuf", pc, 0), ("qbuf", pc, 1), ("qbuf", pc, 2)]
            if pc < 2:
                Q.seg("qT%d" % pc)
                for c in range(4):
                    cc = 4 * pc + c
                    Q.add("pe", lambda e, cc=cc: e.transpose(
                        out=tp[:, cc, :], in_=qbuf[:, 2 * cc:2 * cc + 2, :].rearrange("p h d -> p (h d)"),
                        identity=ident[:]), r=QB + ["const0"], w=["tp"])
                Q.add("act", lambda e, pc=pc: e.copy(out=QT[bp][:, 4 * pc:4 * pc + 4, :],
                                                     in_=tp[:, 4 * pc:4 * pc + 4, :]),
                      r=["tp"], w=[("QT", bp, pc)])
            else:
                Q.add("dve", lambda e: e.tensor_copy(
                    out=kdup[:, 0:nkv, :, :],
                    in_=qbuf[:, 16:16 + nkv, :].unsqueeze(2).to_broadcast([128, nkv, 2, 64])),
                    r=QB, w=["kdup"])
                Q.seg("kT")
                for g in range(nkv):
                    Q.add("pe", lambda e, g=g: e.transpose(
                        out=tp[:, g, :], in_=kdup[:, g, :, :].rearrange("p t d -> p (t d)"),
                        identity=ident[:]), r=["kdup", "const0"], w=["tp"])
                Q.add("act", lambda e: e.copy(out=KT2[k3][:, 0:nkv, :], in_=tp[:, 0:nkv, :]),
                      r=["tp"], w=[("KT2", k3)])
        return Q

    def stage2(p, pi, fb, ws, gi):
        Q = _Rec()
        slot = gi % 4
        bp = gi % 3
        h3 = gi % 4
        kv, kvp = gi % 4, (gi - 1) % 4
        numo = xin[slot][:, 0:1040].rearrange("p (h e) -> p h e", e=65)
        nb = fb % p.bpc
        has_prev = nb > 0
        nkv = p.nkv
        xs = xin[slot][:, 0:D]
        if p.kind in ("A", "G"):
            rep2 = (16 // nkv) // 2
            mprev = mask_pa if p.kind == "A" else mask_pb
            bi = 0
            for g in range(nkv):
                for par in range(2):
                    for cc in range(0, rep2, 2):
                        c0 = g * rep2 + cc
                        sl = bi % 2
                        bi += 1
                        pr = slice(par * 64, par * 64 + 64)
                        Q.seg("qk%d" % (bi - 1))
                        Q.add("pe", lambda e, sl=sl, pr=pr, g=g, c0=c0: e.matmul(
                            sp[sl][:, 1, :], lhsT=KT2[kv][pr, g, :], rhs=QT[bp][pr, c0:c0 + 2, :],
                            start=True, stop=True), r=[("KT2", kv), ("QT", bp, c0 // 4)], w=[("sp", sl, 1)])
                        if has_prev:
                            Q.add("pe", lambda e, sl=sl, pr=pr, g=g, c0=c0: e.matmul(
                                sp[sl][:, 0, :], lhsT=KT2[kvp][pr, g, :], rhs=QT[bp][pr, c0:c0 + 2, :],
                                start=True, stop=True), r=[("KT2", kvp), ("QT", bp, c0 // 4)],
                                w=[("sp", sl, 0)])
                            Q.seg("soft%d" % (bi - 1))
                            Q.add("act", lambda e, sl=sl: e.activation(
                                out=Pb[sl][:], in_=sp[sl][:], func=AF.Exp, scale=0.125),
                                r=[("sp", sl, 0), ("sp", sl, 1)], w=[("Pb", sl, 0), ("Pb", sl, 1)])
                            Q.add("dve", lambda e, sl=sl: e.tensor_tensor(
                                out=Pb[sl][:, 0, :].rearrange("p (h q) -> p h q", q=128),
                                in0=Pb[sl][:, 0, :].rearrange("p (h q) -> p h q", q=128),
                                in1=mprev[:].unsqueeze(1).to_broadcast([128, 2, 128]), op=ALU.mult),
                                r=[("Pb", sl, 0), "const2", "const3"], w=[("Pb", sl, 0)])
                        else:
                            Q.seg("soft%d" % (bi - 1))
                            Q.add("act", lambda e, sl=sl: e.activation(
                                out=Pb[sl][:, 1, :], in_=sp[sl][:, 1, :], func=AF.Exp, scale=0.125),
                                r=[("sp", sl, 1)], w=[("Pb", sl, 1)])
                        Q.add("dve", lambda e, sl=sl: e.tensor_tensor(
                            out=Pb[sl][:, 1, :].rearrange("p (h q) -> p h q", q=128),
                            in0=Pb[sl][:, 1, :].rearrange("p (h q) -> p h q", q=128),
                            in1=mask_cur[:].unsqueeze(1).to_broadcast([128, 2, 128]), op=ALU.mult),
                            r=[("Pb", sl, 1), "const1"], w=[("Pb", sl, 1)])
                        Q.seg("pv%d" % (bi - 1))
                        for i in range(2):
                            h = 2 * (c0 + i) + par
                            if has_prev:
                                Q.add("pe", lambda e, sl=sl, i=i, h=h, g=g: e.matmul(
                                    Ohead(h), lhsT=Pb[sl][:, 0, i * 128:(i + 1) * 128],
                                    rhs=Vp[kvp][:, g, 0:65], start=True, stop=False),
                                    r=[("Pb", sl, 0), ("Vp", kvp), ("Vp1", kvp)], w=[("O", h // 6)])
                            Q.add("pe", lambda e, sl=sl, i=i, h=h, g=g: e.matmul(
                                Ohead(h), lhsT=Pb[sl][:, 1, i * 128:(i + 1) * 128],
                                rhs=Vp[kv][:, g, 0:65], start=(not has_prev), stop=True),
                                r=[("Pb", sl, 1), ("Vp", kv), ("Vp1", kv)], w=[("O", h // 6)])

        Q.seg("epi0")
        if p.kind == "G":
            TH0 = [("xin", slot)]
            for b in range(3):
                ov, h0, nhb = Obank(b)
                if b == 1:
                    Q.add("dve", lambda e, ov=ov, h0=h0, nhb=nhb: e.tensor_copy(
                        out=numo[:, h0:h0 + nhb, :], in_=ov), r=[("O", b)], w=[("numo", b)] + TH0)
                else:
                    Q.add("act", lambda e, ov=ov, h0=h0, nhb=nhb: e.copy(
                        out=numo[:, h0:h0 + nhb, :], in_=ov), r=[("O", b)], w=[("numo", b)] + TH0)
            Q.add("sp", lambda e: e.dma_start(out=p.rows(num[p.g], fb),
                                              in_=numo.rearrange("p h e -> p (h e)")),
                  r=[("numo", b) for b in range(3)] + TH0,
                  w=[("num", p.g, b, p.cls(fb)) for b in p.nat_blocks(fb)], dma="d_numo%d" % slot)
            return Q

        if p.kind == "A":
            for b in range(3):
                ov, h0, nhb = Obank(b)
                Q.add("dve", lambda e, ov=ov, h0=h0, nhb=nhb: e.tensor_tensor(
                    out=lsum[:, h0:h0 + nhb], in0=ov[:, :, 64], in1=esink[ws][:, h0:h0 + nhb],
                    op=ALU.add), r=[("O", b), ("esink", ws)], w=[("lsum", b)])
            srcs = [Obank(b) for b in range(3)]
            srck = [[("O", b)] for b in range(3)]
        else:
            ni = numin
            NK = [("numin", 0), ("numin", 1), ("numin", 2)]
            Q.add("pool", lambda e: e.tensor_tensor(out=ni[:, 0], in0=ni[:, 0], in1=ni[:, 1], op=ALU.add),
                  r=NK[0:2] + BIGK, w=NK[0:1] + BIGK)
            Q.add("pool", lambda e: e.tensor_tensor(out=ni[:, 0], in0=ni[:, 0], in1=ni[:, 2], op=ALU.add),
                  r=[NK[0], NK[2]] + BIGK, w=NK[0:1] + BIGK)
            Q.add("dve", lambda e: e.tensor_copy(out=lsum[:], in_=ni[:, 0, :, 64]), r=NK[0:1] + BIGK,
                  w=[("lsum", 0), ("lsum", 1), ("lsum", 2)])
            srcs = [(ni[:, 0, 0:6, :], 0, 6), (ni[:, 0, 6:12, :], 6, 6), (ni[:, 0, 12:16, :], 12, 4)]
            srck = [NK[0:1] + BIGK] * 3
        LS = [("lsum", 0), ("lsum", 1), ("lsum", 2)]
        Q.add("dve", lambda e: e.reciprocal(out=rl[:], in_=lsum[:]), r=LS, w=["rl"])
        Q.add("dve", lambda e: e.tensor_scalar(out=fsc[:], in0=rl[:], scalar1=rstdh[h3][:], scalar2=None,
                                               op0=ALU.mult), r=["rl", ("rstdh", h3)], w=["fsc"])
        THK = [("th", bp, 0), ("th", bp, 512)]
        for b in range(3):
            ov, h0, nhb = srcs[b]
            Q.add("dve", lambda e, ov=ov, h0=h0, nhb=nhb: e.tensor_tensor(
                out=th[bp][:, h0:h0 + nhb, :], in0=ov[:, :, 0:64], in1=th[bp][:, h0:h0 + nhb, :],
                op=ALU.mult), r=srck[b] + THK, w=THK)
        Q.add("dve", lambda e: e.tensor_tensor(
            out=yb[:].rearrange("p (h d) -> p h d", d=64), in0=th[bp][:],
            in1=fsc[:].unsqueeze(2).to_broadcast([128, 16, 64]), op=ALU.mult),
            r=THK + ["fsc"], w=["yb"])
        Q.seg("epi1")
        for c in range(8):
            Q.add("pe", lambda e, c=c: e.transpose(out=tp[:, c, :], in_=yb[:, c * 128:(c + 1) * 128],
                                                   identity=ident[:]), r=["yb", "const0"], w=["tp"])
        Q.add("act", lambda e: e.copy(out=yT[:], in_=tp[:]), r=["tp"], w=["yT"])
        for si, c0 in enumerate((0, 512)):
            Q.seg("epi%d" % (2 + si))
            z = zp[si]
            for c in range(8):
                Q.add("pe", lambda e, c=c, z=z, c0=c0: e.matmul(
                    z[:], lhsT=yT[:, c, :], rhs=wout[ws][:, c, c0:c0 + 512], start=(c == 0),
                    stop=(c == 7)), r=["yT"] + wkeys("wout", ws, c0, 512, c), w=[("zp", si)])
            Q.add("dve", lambda e, z=z, c0=c0: e.tensor_tensor(
                out=xs[:, c0:c0 + 512], in0=z[:], in1=xs[:, c0:c0 + 512], op=ALU.add),
                r=[("zp", si), ("xin", slot)], w=[("xin", slot)])
        Q.add("sp", lambda e: e.dma_start(out=out[fb * 128:(fb + 1) * 128, :], in_=xs),
              r=[("xin", slot)], w=[("xd", fb)], dma="d_xout%d" % slot)
        return Q

    def play(seg):
        for a, k in seg:
            P.add(*a, **k)

    ORDER = ["2qk0", "2qk1", "1slab0", "2soft0", "2soft1", "DEFERRED", "2pv0", "2qk2", "2pv1", "2qk3", "1c01",
             "2soft2", "2soft3", "1slab1", "2pv2", "2qk4", "2pv3", "2qk5", "1c02", "1c11", "2soft4", "2soft5",
             "1slab2", "2pv4", "2qk6", "2pv5", "2qk7", "1c03", "1c12", "1c21", "2soft6", "2soft7", "1gate0",
             "2pv6", "2pv7", "1c04", "1c13", "1c22", "3preA", "2epi0", "1gate1", "1qT0", "3preT", "2epi1",
             "2epi2", "2epi3"]
    TAIL = ["c14", "c23", "qT1", "c24", "kT"]
    deferred = []

    def flush_deferred():
        for s in deferred:
            play(s)
        del deferred[:]

    def merge(qs):
        names = set()
        for k, q in qs.items():
            names |= set(k + n for n in q.order if q.segs[n] and not (k == "1" and n in TAIL))
        assert names <= set(ORDER), names - set(ORDER)
        for n in ORDER:
            if n == "DEFERRED":
                flush_deferred()
                continue
            q = qs.get(n[0])
            if q is not None:
                play(q.segs.get(n[1:], []))
        q1 = qs.get("1")
        if q1 is not None:
            for n in TAIL:
                if q1.segs.get(n):
                    deferred.append(q1.segs[n])

    GB = [(pi, fb) for pi in range(len(passes)) for fb in range(NBLK)]
    NG = len(GB)

    def args(gi):
        pi, fb = GB[gi]
        return passes[pi], pi, fb

    def full(q):
        for s in q.lst():
            play(s)

    def s1(gi):
        p1, pi1, fb1 = args(gi)
        return stage1(p1, pi1, fb1, pi1 % 2, gi)

    for st in load_steps(passes[0], 0):
        st()
    if marks is not None:
        marks.append((-1, -1, len(P.ops)))
    for g_ in range(3):
        x_load(*args(g_), g_)
    full(stage_pre(*args(0), 0))
    full(s1(0))
    full(stage_pre(*args(1), 1))
    full(s1(1))
    full(stage_pre(*args(2), 2))
    nxt = []
    held = []
    for gi in range(NG):
        p, pi, fb = args(gi)
        ws = pi % 2
        if fb == 0:
            nxt = load_steps(passes[pi + 1], 1 - ws) if pi + 1 < len(passes) else []
        if fb < len(nxt):
            nxt[fb]()
        if gi + 3 < NG:
            x_load(*args(gi + 3), gi + 3)
        qs = {"2": stage2(p, pi, fb, ws, gi)}
        if gi + 2 < NG:
            if p.kind == "C" and GB[gi + 2][0] != pi:
                held.append(gi + 2)
            else:
                qs["1"] = s1(gi + 2)
        hold_pre = bool(held) and fb == NBLK - 1
        if gi + 3 < NG and not hold_pre:
            qs["3"] = stage_pre(*args(gi + 3), gi + 3)
        merge(qs)
        if held and fb == NBLK - 1:
            flush_deferred()
            for h_ in held:
                full(s1(h_))
            del held[:]
            if gi + 3 < NG:
                full(stage_pre(*args(gi + 3), gi + 3))
        if gi + 1 < NG:
            p1, pi1, fb1 = args(gi + 1)
            num_load(p1, pi1, fb1)
        if marks is not None:
            marks.append((pi, fb, len(P.ops)))
    flush_deferred()

    P.emit(final_dma_sems=["d_xout0", "d_xout1", "d_xout2", "d_xout3"], max_ops=max_ops)
    es.close()
    lp.__exit__(None, None, None)
    return nc


def _rope_tables():
    half = 8
    inv = (np.float32(ROPE_THETA) ** (-np.arange(0, 16, 2, dtype=np.float32) / np.float32(16))).astype(np.float32)
    tabs = np.zeros((3, 2, 128, NBLK, half), np.float32)
    p = np.arange(128)
    for oi, (_, dl) in enumerate(B_PAIRS):
        bpc = NBLK // dl
        for fb in range(NBLK):
            r, nb = fb // bpc, fb % bpc
            pos = ((nb * 128 + p) * dl + r).astype(np.float32)
            ang = (pos[:, None] * inv[None, :]).astype(np.float32)
            tabs[oi, 0, :, fb, :] = np.cos(ang.astype(np.float64)).astype(np.float32)
            tabs[oi, 1, :, fb, :] = np.sin(ang.astype(np.float64)).astype(np.float32)
    return tabs.reshape(3, 2, 128, NBLK * half)


def _consts():
    k = np.arange(128)[:, None]
    q = np.arange(128)[None, :]
    c = np.zeros((4, 128, 128), np.float32)
    c[0] = np.eye(128, dtype=np.float32)
    c[1] = (q >= k)
    c[2] = (k > q)
    c[3] = (k >= q)
    return c


def _prep(inputs):
    f = lambda a: np.ascontiguousarray(np.asarray(a, dtype=np.float32))
    rep = lambda v, n: np.repeat(v[:, None, :], n, axis=1)
    shared = {
        "w_in_a": f(inputs["w_in_a"]), "w_out_a": f(inputs["w_out_a"]),
        "w_in_b": f(inputs["w_in_b"]), "w_out_b": f(inputs["w_out_b"]),
        "normw_a": f(np.asarray(inputs["norm_a"]).reshape(2, 8, 128).transpose(0, 2, 1)),
        "normw_b": f(np.asarray(inputs["norm_b"]).reshape(2, 8, 128).transpose(0, 2, 1)),
    }
    qa, ka = np.asarray(inputs["q_gain_a"]), np.asarray(inputs["k_gain_a"])
    ga = np.concatenate([rep(qa, 16), rep(ka, 2)], axis=1).reshape(2, 1, 18 * 64)
    shared["gain_a"] = f(np.broadcast_to(ga, (2, 128, 18 * 64)))
    qb, kb = np.asarray(inputs["q_gain_b"]), np.asarray(inputs["k_gain_b"])
    gb = np.concatenate([np.repeat(qb[:, :, None, :], 16, axis=2), np.repeat(kb[:, :, None, :], 4, axis=2)],
                        axis=2).reshape(2, 3, 1, 20 * 64)
    shared["gain_b"] = f(np.broadcast_to(gb, (2, 3, 128, 20 * 64)))
    shared["sinks"] = f(np.broadcast_to(np.asarray(inputs["sinks_a"])[:, None, :], (2, 128, 16)))
    shared["rope"] = _rope_tables()
    shared["consts"] = _consts()
    x = np.asarray(inputs["x"], dtype=np.float32)
    return [dict(shared, x=np.ascontiguousarray(x[b])) for b in range(x.shape[0])]


_NC_CACHE = {}


def kernel(**inputs):
    in_maps = _prep(inputs)
    if 4 not in _NC_CACHE:
        _NC_CACHE[4] = build(4)
    res = run_bass_kernel_spmd(_NC_CACHE[4], in_maps, core_ids=list(range(8)))
    return np.stack([np.asarray(r["out"], dtype=np.float32) for r in res.results], axis=0)
```

```python
import numpy as np
from contextlib import ExitStack
import concourse.bass as bass
import concourse.mybir as mybir
from concourse.bass_utils import run_bass_kernel_spmd

F32 = mybir.dt.float32
BF16 = mybir.dt.bfloat16
AF = mybir.ActivationFunctionType
ALU = mybir.AluOpType
AX = mybir.AxisListType

S = 4096
D = 1024
NBLK = 32
EPS = 1e-6
ROPE_THETA = 500000.0
B_PAIRS = ((128, 1), (512, 4), (2048, 16))


class _Op:
    __slots__ = ("eng", "fn", "deps", "dma", "signal", "seq", "sem")

    def __init__(self, eng, fn, deps, dma):
        self.eng, self.fn, self.deps, self.dma = eng, fn, deps, dma
        self.signal = False
        self.seq = 0
        self.sem = None


class Prog:
    def __init__(self, nc):
        self.nc = nc
        self.ops = []
        self.last_w = {}
        self.readers = {}

    def add(self, eng, fn, r=(), w=(), dma=None):
        i = len(self.ops)
        deps = set()
        for k in r:
            j = self.last_w.get(k)
            if j is not None:
                deps.add(j)
        for k in w:
            j = self.last_w.get(k)
            if j is not None:
                deps.add(j)
            for x in self.readers.get(k, ()):
                deps.add(x)
        self.ops.append(_Op(eng, fn, deps, dma))
        for k in r:
            self.readers.setdefault(k, []).append(i)
        for k in w:
            self.last_w[k] = i
            self.readers[k] = []
        return i

    class _Fake:
        def __init__(self):
            self.name, self.kw, self.args = None, {}, ()

        def __getattr__(self, name):
            def f(*a, **k):
                self.name, self.args, self.kw = name, a, k
                return self
            return f

    def _cost(self, op):
        rec = Prog._Fake()
        op.fn(rec)
        out = rec.kw.get("out", rec.args[0] if rec.args else None)
        try:
            n = int(out.free_size())
        except Exception:
            n = 256
        e = op.eng
        if op.dma is not None:
            return 0.12
        if e == "pe":
            return 0.05 + n / 1500.0
        if e == "act":
            return 0.12 + n / 1100.0 + (0.1 if "accum_out" in rec.kw else 0.0)
        if e == "dve":
            return 0.08 + n / 1000.0
        if e == "pool":
            if rec.kw.get("op", None) == ALU.pow:
                return 0.1 + 0.2 * n
            return 0.15 + n / 430.0
        return 0.1

    def schedule(self, window=6000):
        import heapq
        ops = self.ops
        n = len(ops)
        succ = [[] for _ in range(n)]
        indeg = [0] * n
        for i, op in enumerate(ops):
            for d in op.deps:
                succ[d].append(i)
                indeg[i] += 1
        cost = [self._cost(op) for op in ops]
        engs = ["pe", "act", "dve", "pool", "sp"]
        eng_t = {e: 0.0 for e in engs}
        fin = [0.0] * n
        ready_t = [0.0] * n
        wait_h = {e: [] for e in engs}
        avail_h = {e: [] for e in engs}
        for i in range(n):
            if indeg[i] == 0:
                heapq.heappush(wait_h[ops[i].eng], (0.0, i))
        order = []
        done = [False] * n
        lo = 0
        while len(order) < n:
            while lo < n and done[lo]:
                lo += 1
            best = None
            for e in engs:
                wh, ah = wait_h[e], avail_h[e]
                while wh and wh[0][0] <= eng_t[e]:
                    heapq.heappush(ah, heapq.heappop(wh)[1])
                cand = None
                if ah:
                    cand = (eng_t[e], ah[0], 0)
                elif wh:
                    cand = (wh[0][0], wh[0][1], 1)
                if cand is not None and cand[1] < lo + window:
                    if best is None or (cand[0], cand[1]) < (best[0], best[1]):
                        best = cand + (e,)
            if best is None:
                c = [(h[0] if isinstance(h[0], int) else h[0][1]) for e in engs
                     for h in (avail_h[e], wait_h[e]) if h]
                i = min(c)
                e = ops[i].eng
                if i in avail_h[e]:
                    avail_h[e].remove(i)
                    heapq.heapify(avail_h[e])
                else:
                    wait_h[e] = [x for x in wait_h[e] if x[1] != i]
                    heapq.heapify(wait_h[e])
                start = max(eng_t[e], ready_t[i])
            else:
                start, i, src, e = best
                if src == 0:
                    heapq.heappop(avail_h[e])
                else:
                    heapq.heappop(wait_h[e])
            op = ops[i]
            eng_t[e] = start + cost[i]
            fin[i] = start + cost[i] + (2.5 if op.dma is not None else 0.0)
            done[i] = True
            order.append(i)
            for s in succ[i]:
                lat = 0.05 if (ops[s].eng == e and op.dma is None) else 0.3
                ready_t[s] = max(ready_t[s], fin[i] + lat)
                indeg[s] -= 1
                if indeg[s] == 0:
                    heapq.heappush(wait_h[ops[s].eng], (ready_t[s], s))
        pos = {old: new for new, old in enumerate(order)}
        new_ops = [ops[i] for i in order]
        for op in new_ops:
            op.deps = set(pos[d] for d in op.deps)
        self.ops = new_ops
        self.est_us = max(eng_t.values())

    def emit(self, final_dma_sems=(), max_ops=None):
        nc = self.nc
        ops = self.ops if max_ops is None else self.ops[:max_ops]

        def skip(op, dop):
            return op.eng == "pe" and dop.eng == "pe" and dop.dma is None and op.dma is None

        for op in ops:
            for d in op.deps:
                if not skip(op, ops[d]):
                    ops[d].signal = True
        cnt = {}
        dma_names = []
        for op in ops:
            if op.dma is not None:
                if op.dma not in cnt:
                    dma_names.append(op.dma)
                cnt[op.dma] = cnt.get(op.dma, 0) + 16
                op.seq = cnt[op.dma]
                op.sem = op.dma
            elif op.signal:
                cnt[op.eng] = cnt.get(op.eng, 0) + 1
                op.seq = cnt[op.eng]
                op.sem = op.eng
        with ExitStack() as es:
            sems = {}
            for name in ["pe", "act", "dve", "pool"] + dma_names:
                sems[name] = es.enter_context(nc.semaphore("s_" + name))
            block = es.enter_context(nc.Block())

            def run(engname, eng):
                waited = {}
                for op in ops:
                    if op.eng != engname:
                        continue
                    needs = {}
                    for d in op.deps:
                        dop = ops[d]
                        if skip(op, dop):
                            continue
                        if needs.get(dop.sem, 0) < dop.seq:
                            needs[dop.sem] = dop.seq
                    for sname, val in needs.items():
                        if waited.get(sname, 0) < val:
                            eng.wait_ge(sems[sname], val)
                            waited[sname] = val
                    ins = op.fn(eng)
                    if op.dma is not None:
                        ins.then_inc(sems[op.dma], 16)
                    elif op.signal:
                        ins.then_inc(sems[engname], 1)
                if engname == "sp":
                    for name in final_dma_sems:
                        if name in cnt:
                            eng.wait_ge(sems[name], cnt[name])

            @block.tensor
            def _(e):
                run("pe", e)

            @block.scalar
            def _(e):
                run("act", e)

            @block.vector
            def _(e):
                run("dve", e)

            @block.gpsimd
            def _(e):
                run("pool", e)

            @block.sync
            def _(e):
                run("sp", e)


class Pass:
    def __init__(self, kind, layer, idx, g=0):
        self.kind, self.layer, self.idx, self.g = kind, layer, idx, g
        if kind == "A":
            self.delta, self.nkv, self.ncols = 1, 2, 2304
            self.slabs = [(0, 512), (512, 512), (1024, 256), (1280, 512), (1792, 512)]
            self.wname, self.wcol0 = "w_in_a", 0
        elif kind == "G":
            self.delta, self.nkv, self.ncols = B_PAIRS[g][1], 4, 1536
            self.slabs = [(0, 512), (512, 512), (1024, 512)]
            self.wname, self.wcol0 = "w_in_b", 1536 * g
        else:
            self.delta, self.nkv, self.ncols = 1, 0, 1024
            self.slabs = [(0, 512), (512, 512)]
            self.wname, self.wcol0 = "w_in_b", 4608
        self.nh = 16 + self.nkv
        self.has_out = kind in ("A", "C")
        self.bpc = NBLK // self.delta

    def rows(self, t, fb):
        if self.delta == 1:
            return t[fb * 128:(fb + 1) * 128, :]
        r, nb = fb // self.bpc, fb % self.bpc
        return t.rearrange("(j r) e -> r j e", r=self.delta)[r, nb * 128:(nb + 1) * 128, :]

    def nat_blocks(self, fb):
        if self.delta == 1:
            return [fb]
        nb = fb % self.bpc
        return list(range(nb * self.delta, (nb + 1) * self.delta))

    def cls(self, fb):
        return fb // self.bpc


def make_passes(n_layers):
    ps = []
    for layer in range(n_layers):
        idx = layer // 2
        if layer % 2 == 0:
            ps.append(Pass("A", layer, idx))
        else:
            for g in range(3):
                ps.append(Pass("G", layer, idx, g))
            ps.append(Pass("C", layer, idx))
    return ps


class _Rec:
    def __init__(self):
        self.segs = {}
        self.order = []
        self.cur = None
        self.seg("pre")

    def add(self, *a, **k):
        self.segs[self.cur].append((a, k))

    def seg(self, name):
        self.cur = name
        if name not in self.segs:
            self.segs[name] = []
            self.order.append(name)

    def lst(self):
        return [self.segs[n] for n in self.order if self.segs[n]]


def build(n_layers=4, max_ops=None, marks=None, sched=True):
    nc = bass.Bass("TRN2", target_bir_lowering=False)
    dr = {}

    def din(name, shape):
        dr[name] = nc.dram_tensor(name, list(shape), F32, kind="ExternalInput").ap()

    din("x", (S, D))
    din("w_in_a", (2, D, 2304))
    din("w_out_a", (2, D, D))
    din("w_in_b", (2, D, 5632))
    din("w_out_b", (2, D, D))
    din("normw_a", (2, 128, 8))
    din("normw_b", (2, 128, 8))
    din("gain_a", (2, 128, 18 * 64))
    din("gain_b", (2, 3, 128, 20 * 64))
    din("sinks", (2, 128, 16))
    din("rope", (3, 2, 128, NBLK * 8))
    din("consts", (4, 128, 128))
    out = nc.dram_tensor("out", [S, D], F32, kind="ExternalOutput").ap()
    num = [nc.dram_tensor("num%d" % g, [S, 1040], F32, kind="Internal").ap() for g in range(3)]

    P = Prog(nc)
    es = ExitStack()
    lp = nc.allow_low_precision("bf16 matmul operands, fp32 accumulation (problem tolerance)")
    lp.__enter__()

    def sb(name, shape, dt=F32):
        return es.enter_context(nc.sbuf_tensor(name, list(shape), dt))

    def ps(name, shape, dt=F32):
        return es.enter_context(nc.psum_tensor(name, list(shape), dt))

    ident = sb("ident", [128, 128], BF16)
    mask_cur = sb("mask_cur", [128, 128], BF16)
    mask_pa = sb("mask_pa", [128, 128], BF16)
    mask_pb = sb("mask_pb", [128, 128], BF16)
    negh = sb("negh", [128, 32])
    win = [sb("win%d" % i, [128, 8, 2304], BF16) for i in range(2)]
    wout = [sb("wout%d" % i, [128, 8, 1024], BF16) for i in range(2)]
    wstage = [sb("wstage%d" % i, [128, 8, 128]) for i in range(2)]
    normw = [sb("normw%d" % i, [128, 8]) for i in range(2)]
    gain = [sb("gain%d" % i, [128, 20, 64]) for i in range(2)]
    ropec = [sb("ropec%d" % i, [128, NBLK, 8]) for i in range(2)]
    ropes = [sb("ropes%d" % i, [128, NBLK, 8]) for i in range(2)]
    sinkraw = [sb("sinkraw%d" % i, [128, 16]) for i in range(2)]
    esink = [sb("esink%d" % i, [128, 16]) for i in range(2)]
    xin = [sb("xin%d" % i, [128, 1040]) for i in range(4)]
    xb = sb("xb", [128, D], BF16)
    ss = sb("ss", [128, 1])
    rt = sb("rt", [128, 1])
    rstd = [sb("rstd%d" % i, [128, 1]) for i in range(2)]
    rstdh = [sb("rstdh%d" % i, [128, 1]) for i in range(4)]
    xT = [sb("xT%d" % i, [128, 8, 128], BF16) for i in range(2)]
    big = sb("big", [128, 3200])
    zqk = big[:, 0:1280].rearrange("p (h d) -> p h d", d=64)
    sq = big[:, 1280:2560].rearrange("p (h d) -> p h d", d=64)
    numin = big[:, 0:3120].rearrange("p (g h e) -> p g h e", g=3, e=65)
    BIGK = [("zqk", 0), ("zqk", 1), ("zqk", 2), ("sq", 0), ("sq", 1), ("sq", 2)]
    ssh = sb("ssh", [128, 20])
    rht = sb("rht", [128, 20])
    rh = sb("rh", [128, 20])
    ra = sb("ra", [128, 20, 8])
    rb = sb("rb", [128, 20, 8])
    rc = sb("rc", [128, 20, 8])
    rd = sb("rd", [128, 20, 8])
    qbuf = sb("qbuf", [128, 20, 64], BF16)
    kdup = sb("kdup", [128, 4, 2, 64], BF16)
    QT = [sb("QT%d" % i, [128, 8, 128], BF16) for i in range(3)]
    KT2 = [sb("KT2_%d" % i, [128, 4, 128], BF16) for i in range(4)]
    Vp = [sb("Vp%d" % i, [128, 4, 66], BF16) for i in range(4)]
    th = [sb("th%d" % i, [128, 16, 64]) for i in range(3)]
    Pb = [sb("Pb%d" % i, [128, 2, 256], BF16) for i in range(2)]
    lsum = sb("lsum", [128, 16])
    rl = sb("rl", [128, 16])
    fsc = sb("fsc", [128, 16])
    yb = sb("yb", [128, D], BF16)
    yT = sb("yT", [128, 8, 128], BF16)
    tp = ps("tp", [128, 8, 128], BF16)
    zp = [ps("zp%d" % i, [128, 512]) for i in range(2)]
    sp = [ps("sp%d" % i, [128, 2, 256]) for i in range(2)]
    Op = ps("Op", [128, 3, 512])

    def Ohead(h):
        return Op[:, h // 6, (h % 6) * 65:(h % 6) * 65 + 65]

    def Obank(b):
        nhb = 6 if b < 2 else 4
        return Op[:, b, 0:nhb * 65].rearrange("p (h e) -> p h e", e=65), b * 6, nhb

    cstage = xin[3][:, 0:512].rearrange("p (c n) -> p c n", n=128)
    P.add("sp", lambda e: e.dma_start(out=cstage, in_=dr["consts"].rearrange("c p n -> p c n")),
          w=[("xin", 3)], dma="d_const")
    for i, t in enumerate([ident, mask_cur, mask_pa, mask_pb]):
        P.add("dve", lambda e, i=i, t=t: e.tensor_copy(out=t[:], in_=cstage[:, i, :]),
              r=[("xin", 3)], w=["const%d" % i])
    P.add("pool", lambda e: e.memset(negh[:], -0.5), w=["negh"])
    for s_ in range(4):
        P.add("pool", lambda e, s_=s_: e.memset(Vp[s_][:, :, 64:66], 1.0), w=[("Vp1", s_)])

    passes = make_passes(n_layers)

    def wkeys(name, slot, c0, wd, c):
        return [(name, slot, u, c) for u in range(c0 // 128, (c0 + wd) // 128)]

    def load_steps(p, slot):
        steps = []
        L = p.idx
        wsrc = dr[p.wname][L].rearrange("(c q) n -> q c n", q=128)
        nw = dr["normw_a" if p.kind == "A" else "normw_b"][L]

        def small():
            P.add("sp", lambda e: e.dma_start(out=normw[slot][:], in_=nw), w=[("normw", slot)],
                  dma="d_normw%d" % slot)
            if p.kind != "C":
                gsrc = dr["gain_a"][L] if p.kind == "A" else dr["gain_b"][L, p.g]
                P.add("sp", lambda e: e.dma_start(
                    out=gain[slot][:, 0:p.nh, :].rearrange("p h d -> p (h d)"), in_=gsrc),
                    w=[("gain", slot)], dma="d_gain%d" % slot)
                oi = {1: 0, 4: 1, 16: 2}[p.delta]
                P.add("sp", lambda e: e.dma_start(
                    out=ropec[slot][:].rearrange("p b f -> p (b f)"), in_=dr["rope"][oi, 0]),
                    w=[("ropec", slot)], dma="d_ropec%d" % slot)
                P.add("sp", lambda e: e.dma_start(
                    out=ropes[slot][:].rearrange("p b f -> p (b f)"), in_=dr["rope"][oi, 1]),
                    w=[("ropes", slot)], dma="d_ropes%d" % slot)
            if p.kind == "A":
                P.add("sp", lambda e: e.dma_start(out=sinkraw[slot][:], in_=dr["sinks"][L]),
                      w=[("sinkraw", slot)], dma="d_sink%d" % slot)
                P.add("act", lambda e: e.activation(out=esink[slot][:], in_=sinkraw[slot][:],
                                                    func=AF.Exp),
                      r=[("sinkraw", slot)], w=[("esink", slot)])

        steps.append(small)
        slabs = [("in", c0, 128) for c0 in range(0, p.ncols, 128)]
        if p.has_out:
            osrc = dr["w_out_a" if p.kind == "A" else "w_out_b"][L].rearrange(
                "(c q) n -> q c n", q=128)
            slabs += [("out", c0, 128) for c0 in range(0, 1024, 128)]

        def dma_slab(j):
            kind, c0, wd = slabs[j]
            st = j % 2
            if kind == "in":
                P.add("sp", lambda e: e.dma_start(
                    out=wstage[st][:, :, 0:wd], in_=wsrc[:, :, p.wcol0 + c0:p.wcol0 + c0 + wd]),
                    w=[("wstage", st)], dma="d_wst%d" % st)
            else:
                P.add("sp", lambda e: e.dma_start(out=wstage[st][:], in_=osrc[:, :, c0:c0 + 128]),
                      w=[("wstage", st)], dma="d_wst%d" % st)

        def cast_slab(j):
            kind, c0, wd = slabs[j]
            st = j % 2
            for c in range(8):
                if kind == "in":
                    P.add("dve", lambda e, c=c: e.tensor_scalar(
                        out=win[slot][:, c, c0:c0 + wd], in0=wstage[st][:, c, 0:wd],
                        scalar1=normw[slot][:, c:c + 1], scalar2=None, op0=ALU.mult),
                        r=[("wstage", st), ("normw", slot)], w=wkeys("win", slot, c0, wd, c))
                else:
                    P.add("act", lambda e, c=c: e.copy(
                        out=wout[slot][:, c, c0:c0 + 128], in_=wstage[st][:, c, :]),
                        r=[("wstage", st)], w=wkeys("wout", slot, c0, 128, c))

        def mk(j):
            def f():
                if j < len(slabs):
                    dma_slab(j)
                if j >= 1:
                    cast_slab(j - 1)
            return f

        for j in range(len(slabs) + 1):
            steps.append(mk(j))
        return steps

    def x_load(p, pi, fb, gi):
        slot = gi % 4
        src = dr["x"] if pi == 0 else out
        P.add("sp", lambda e: e.dma_start(out=xin[slot][:, 0:D], in_=p.rows(src, fb)),
              r=[("xd", b) for b in p.nat_blocks(fb)], w=[("xin", slot)], dma="d_xin%d" % slot)

    def num_load(p, pi, fb):
        if p.kind == "C":
            for g in range(3):
                dlt = B_PAIRS[g][1]
                P.add("sp", lambda e, g=g: e.dma_start(
                    out=numin[:, g].rearrange("p h e -> p (h e)"),
                    in_=num[g][fb * 128:(fb + 1) * 128, :]),
                    r=[("num", g, fb, r_) for r_ in range(dlt)], w=[("numin", g)] + BIGK,
                    dma="d_numin%d" % g)

    def stage_pre(p, pi, fb, gi):
        Q = _Rec()
        slot = gi % 4
        xs = xin[slot][:, 0:D]
        r2, h3, t2 = gi % 2, gi % 4, gi % 2
        Q.seg("preA")
        Q.add("act", lambda e: e.activation(out=xb[:], in_=xs, func=AF.Square, accum_out=ss[:]),
              r=[("xin", slot)], w=["xb", "ss"])
        Q.add("act", lambda e: e.copy(out=xb[:], in_=xs), r=[("xin", slot)], w=["xb"])
        Q.add("pool", lambda e: e.tensor_scalar(out=rt[:], in0=ss[:], scalar1=1.0 / D, scalar2=EPS,
                                                op0=ALU.mult, op1=ALU.add), r=["ss"], w=["rt"])
        Q.add("pool", lambda e: e.tensor_tensor(out=rstd[r2][:], in0=rt[:], in1=negh[:, 0:1], op=ALU.pow),
              r=["rt", "negh"], w=[("rstd", r2)])
        if p.kind != "G":
            Q.add("pool", lambda e: e.tensor_scalar(out=rstdh[h3][:], in0=rstd[r2][:], scalar1=0.5,
                                                    scalar2=1.0, op0=ALU.mult, op1=ALU.mult),
                  r=[("rstd", r2)], w=[("rstdh", h3)])
        Q.seg("preT")
        for c in range(8):
            Q.add("pe", lambda e, c=c: e.transpose(out=tp[:, c, :], in_=xb[:, c * 128:(c + 1) * 128],
                                                   identity=ident[:]),
                  r=["xb", "const0"], w=["tp"])
        Q.add("act", lambda e: e.copy(out=xT[t2][:], in_=tp[:]), r=["tp"], w=[("xT", t2)])
        return Q

    def stage1(p, pi, fb, ws, gi):
        Q = _Rec()
        bp = gi % 3
        k3 = gi % 4
        r2, h3, xts = gi % 2, gi % 4, gi % 2
        nh, nkv = p.nh, p.nkv

        def slab_mm(si, c0, wd):
            z = zp[si % 2]
            for c in range(8):
                Q.add("pe", lambda e, c=c: e.matmul(z[:, 0:wd], lhsT=xT[xts][:, c, :],
                                                    rhs=win[ws][:, c, c0:c0 + wd],
                                                    start=(c == 0), stop=(c == 7)),
                      r=[("xT", xts)] + wkeys("win", ws, c0, wd, c), w=[("zp", si % 2)])
            return z

        zflat = big[:, 0:1280]
        thf = th[bp][:].rearrange("p h d -> p (h d)")
        if p.kind in ("A", "G"):
            for si in range(2):
                c0, wd = p.slabs[si]
                Q.seg("slab%d" % si)
                z = slab_mm(si, c0, wd)
                Q.add("dve", lambda e, z=z, c0=c0, wd=wd: e.tensor_scalar(
                    out=zflat[:, c0:c0 + wd], in0=z[:, 0:wd], scalar1=rstd[r2][:], scalar2=None,
                    op0=ALU.mult), r=[("zp", si % 2), ("rstd", r2)], w=[("zqk", si)])
            c0, wd = p.slabs[2]
            Q.seg("slab2")
            z = slab_mm(2, c0, wd)
            kw = 64 * nkv
            Q.add("dve", lambda e, z=z: e.tensor_scalar(
                out=zflat[:, 1024:1024 + kw], in0=z[:, 0:kw], scalar1=rstd[r2][:], scalar2=None,
                op0=ALU.mult), r=[("zp", 0), ("rstd", r2)], w=[("zqk", 2)])
            Q.add("dve", lambda e, z=z: e.tensor_scalar(
                out=Vp[k3][:, 0:nkv, 0:64], in0=z[:, kw:2 * kw].rearrange("p (g d) -> p g d", d=64),
                scalar1=rstd[r2][:], scalar2=None, op0=ALU.mult), r=[("zp", 0), ("rstd", r2)],
                w=[("Vp", k3)])
            gslabs = [(3, p.slabs[3]), (4, p.slabs[4])] if p.kind == "A" else []
            gbase = 1280
        else:
            gslabs = [(0, p.slabs[0]), (1, p.slabs[1])]
            gbase = 0
        for gi, (si, (c0, wd)) in enumerate(gslabs):
            Q.seg("gate%d" % gi)
            z = slab_mm(si, c0, wd)
            g0 = c0 - gbase
            Q.add("act", lambda e, z=z, g0=g0, wd=wd: e.activation(
                out=thf[:, g0:g0 + wd], in_=z[:, 0:wd], func=AF.Tanh, scale=rstdh[h3][:]),
                r=[("zp", si % 2), ("rstdh", h3)], w=[("th", bp, g0)])
            Q.add("dve", lambda e, z=z, g0=g0, wd=wd: e.scalar_tensor_tensor(
                out=thf[:, g0:g0 + wd], in0=thf[:, g0:g0 + wd], scalar=1.0, in1=z[:, 0:wd],
                op0=ALU.add, op1=ALU.mult), r=[("zp", si % 2), ("th", bp, g0)], w=[("th", bp, g0)])
        if p.kind == "C":
            return Q

        pieces = [(0, 0, 8), (1, 8, 16), (2, 16, 16 + nkv)]
        for pc, h0, h1 in pieces:
            n = h1 - h0
            ZK = [("zqk", pc)]
            Q.seg("c%d1" % pc)
            Q.add("act", lambda e, h0=h0, h1=h1: e.activation(out=sq[:, h0:h1, :], in_=zqk[:, h0:h1, :],
                                                              func=AF.Square), r=ZK, w=[("sq", pc)])
            Q.add("dve", lambda e, h0=h0, h1=h1: e.tensor_reduce(out=ssh[:, h0:h1], in_=sq[:, h0:h1, :],
                                                                 axis=AX.X, op=ALU.add),
                  r=[("sq", pc)], w=[("ssh", pc)])
            Q.seg("c%d2" % pc)
            Q.add("pool", lambda e, h0=h0, h1=h1: e.tensor_scalar(
                out=rht[:, h0:h1], in0=ssh[:, h0:h1], scalar1=1.0 / 64, scalar2=EPS, op0=ALU.mult,
                op1=ALU.add), r=[("ssh", pc)], w=[("rht", pc)])
            Q.add("pool", lambda e, h0=h0, h1=h1: e.tensor_tensor(
                out=rh[:, h0:h1], in0=rht[:, h0:h1], in1=negh[:, h0:h1], op=ALU.pow),
                r=[("rht", pc), "negh"], w=[("rh", pc)])
            Q.seg("c%d3" % pc)
            Q.add("dve", lambda e, h0=h0, h1=h1, n=n: e.tensor_tensor(
                out=zqk[:, h0:h1, :], in0=zqk[:, h0:h1, :],
                in1=rh[:, h0:h1].unsqueeze(2).to_broadcast([128, n, 64]), op=ALU.mult),
                r=ZK + [("rh", pc)], w=ZK)
            Q.add("dve", lambda e, h0=h0, h1=h1: e.tensor_tensor(
                out=zqk[:, h0:h1, 0:16], in0=zqk[:, h0:h1, 0:16], in1=gain[ws][:, h0:h1, 0:16],
                op=ALU.mult), r=ZK + [("gain", ws)], w=ZK)
            Q.add("dve", lambda e, h0=h0, h1=h1: e.tensor_tensor(
                out=qbuf[:, h0:h1, 16:64], in0=zqk[:, h0:h1, 16:64], in1=gain[ws][:, h0:h1, 16:64],
                op=ALU.mult), r=ZK + [("gain", ws)], w=[("qbuf", pc, 2)])
            Q.seg("c%d4" % pc)
            cb = ropec[ws][:, fb, :].unsqueeze(1).to_broadcast([128, n, 8])
            sbb = ropes[ws][:, fb, :].unsqueeze(1).to_broadcast([128, n, 8])
            t1 = zqk[:, h0:h1, 0:8]
            t2 = zqk[:, h0:h1, 8:16]
            RR = ZK + [("ropec", ws), ("ropes", ws)]
            for nm, tt, a_, b_ in (("ra", ra, t1, cb), ("rb", rb, t2, sbb), ("rc", rc, t2, cb),
                                   ("rd", rd, t1, sbb)):
                Q.add("pool", lambda e, tt=tt, a_=a_, b_=b_, h0=h0, h1=h1: e.tensor_tensor(
                    out=tt[:, h0:h1, :], in0=a_, in1=b_, op=ALU.mult), r=RR, w=[(nm, pc)])
            Q.add("pool", lambda e, h0=h0, h1=h1: e.tensor_tensor(
                out=qbuf[:, h0:h1, 0:8], in0=ra[:, h0:h1, :], in1=rb[:, h0:h1, :], op=ALU.subtract),
                r=[("ra", pc), ("rb", pc)], w=[("qbuf", pc, 0)])
            Q.add("pool", lambda e, h0=h0, h1=h1: e.tensor_tensor(
                out=qbuf[:, h0:h1, 8:16], in0=rc[:, h0:h1, :], in1=rd[:, h0:h1, :], op=ALU.add),
                r=[("rc", pc), ("rd", pc)], w=[("qbuf", pc, 1)])
            QB = [("qbuf", pc, 0), ("qbuf", pc, 1), ("qbuf", pc, 2)]
            if pc < 2:
                Q.seg("qT%d" % pc)
                for c in range(4):
                    cc = 4 * pc + c
                    Q.add("pe", lambda e, cc=cc: e.transpose(
                        out=tp[:, cc, :], in_=qbuf[:, 2 * cc:2 * cc + 2, :].rearrange("p h d -> p (h d)"),
                        identity=ident[:]), r=QB + ["const0"], w=["tp"])
                Q.add("act", lambda e, pc=pc: e.copy(out=QT[bp][:, 4 * pc:4 * pc + 4, :],
                                                     in_=tp[:, 4 * pc:4 * pc + 4, :]),
                      r=["tp"], w=[("QT", bp, pc)])
            else:
                Q.add("dve", lambda e: e.tensor_copy(
                    out=kdup[:, 0:nkv, :, :],
                    in_=qbuf[:, 16:16 + nkv, :].unsqueeze(2).to_broadcast([128, nkv, 2, 64])),
                    r=QB, w=["kdup"])
                Q.seg("kT")
                for g in range(nkv):
                    Q.add("pe", lambda e, g=g: e.transpose(
                        out=tp[:, g, :], in_=kdup[:, g, :, :].rearrange("p t d -> p (t d)"),
                        identity=ident[:]), r=["kdup", "const0"], w=["tp"])
                Q.add("act", lambda e: e.copy(out=KT2[k3][:, 0:nkv, :], in_=tp[:, 0:nkv, :]),
                      r=["tp"], w=[("KT2", k3)])
        return Q

    def stage2(p, pi, fb, ws, gi):
        Q = _Rec()
        slot = gi % 4
        bp = gi % 3
        h3 = gi % 4
        kv, kvp = gi % 4, (gi - 1) % 4
        numo = xin[slot][:, 0:1040].rearrange("p (h e) -> p h e", e=65)
        nb = fb % p.bpc
        has_prev = nb > 0
        nkv = p.nkv
        xs = xin[slot][:, 0:D]
        if p.kind in ("A", "G"):
            rep2 = (16 // nkv) // 2
            mprev = mask_pa if p.kind == "A" else mask_pb
            bi = 0
            for g in range(nkv):
                for par in range(2):
                    for cc in range(0, rep2, 2):
                        c0 = g * rep2 + cc
                        sl = bi % 2
                        bi += 1
                        pr = slice(par * 64, par * 64 + 64)
                        Q.seg("qk%d" % (bi - 1))
                        Q.add("pe", lambda e, sl=sl, pr=pr, g=g, c0=c0: e.matmul(
                            sp[sl][:, 1, :], lhsT=KT2[kv][pr, g, :], rhs=QT[bp][pr, c0:c0 + 2, :],
                            start=True, stop=True), r=[("KT2", kv), ("QT", bp, c0 // 4)], w=[("sp", sl)])
                        if has_prev:
                            Q.add("pe", lambda e, sl=sl, pr=pr, g=g, c0=c0: e.matmul(
                                sp[sl][:, 0, :], lhsT=KT2[kvp][pr, g, :], rhs=QT[bp][pr, c0:c0 + 2, :],
                                start=True, stop=True), r=[("KT2", kvp), ("QT", bp, c0 // 4)],
                                w=[("sp", sl)])
                            Q.seg("soft%d" % (bi - 1))
                            Q.add("act", lambda e, sl=sl: e.activation(
                                out=Pb[sl][:], in_=sp[sl][:], func=AF.Exp, scale=0.125),
                                r=[("sp", sl), ("sp", sl)], w=[("Pb", sl, 0), ("Pb", sl, 1)])
                            Q.add("dve", lambda e, sl=sl: e.tensor_tensor(
                                out=Pb[sl][:, 0, :].rearrange("p (h q) -> p h q", q=128),
                                in0=Pb[sl][:, 0, :].rearrange("p (h q) -> p h q", q=128),
                                in1=mprev[:].unsqueeze(1).to_broadcast([128, 2, 128]), op=ALU.mult),
                                r=[("Pb", sl, 0), "const2", "const3"], w=[("Pb", sl, 0)])
                        else:
                            Q.seg("soft%d" % (bi - 1))
                            Q.add("act", lambda e, sl=sl: e.activation(
                                out=Pb[sl][:, 1, :], in_=sp[sl][:, 1, :], func=AF.Exp, scale=0.125),
                                r=[("sp", sl)], w=[("Pb", sl, 1)])
                        Q.add("dve", lambda e, sl=sl: e.tensor_tensor(
                            out=Pb[sl][:, 1, :].rearrange("p (h q) -> p h q", q=128),
                            in0=Pb[sl][:, 1, :].rearrange("p (h q) -> p h q", q=128),
                            in1=mask_cur[:].unsqueeze(1).to_broadcast([128, 2, 128]), op=ALU.mult),
                            r=[("Pb", sl, 1), "const1"], w=[("Pb", sl, 1)])
                        Q.seg("pv%d" % (bi - 1))
                        for i in range(2):
                            h = 2 * (c0 + i) + par
                            if has_prev:
                                Q.add("pe", lambda e, sl=sl, i=i, h=h, g=g: e.matmul(
                                    Ohead(h), lhsT=Pb[sl][:, 0, i * 128:(i + 1) * 128],
                                    rhs=Vp[kvp][:, g, 0:65], start=True, stop=False),
                                    r=[("Pb", sl, 0), ("Vp", kvp), ("Vp1", kvp)], w=[("O", h // 6)])
                            Q.add("pe", lambda e, sl=sl, i=i, h=h, g=g: e.matmul(
                                Ohead(h), lhsT=Pb[sl][:, 1, i * 128:(i + 1) * 128],
                                rhs=Vp[kv][:, g, 0:65], start=(not has_prev), stop=True),
                                r=[("Pb", sl, 1), ("Vp", kv), ("Vp1", kv)], w=[("O", h // 6)])

        Q.seg("epi0")
        if p.kind == "G":
            TH0 = [("xin", slot)]
            for b in range(3):
                ov, h0, nhb = Obank(b)
                if b == 1:
                    Q.add("dve", lambda e, ov=ov, h0=h0, nhb=nhb: e.tensor_copy(
                        out=numo[:, h0:h0 + nhb, :], in_=ov), r=[("O", b)], w=[("numo", b)] + TH0)
                else:
                    Q.add("act", lambda e, ov=ov, h0=h0, nhb=nhb: e.copy(
                        out=numo[:, h0:h0 + nhb, :], in_=ov), r=[("O", b)], w=[("numo", b)] + TH0)
            Q.add("sp", lambda e: e.dma_start(out=p.rows(num[p.g], fb),
                                              in_=numo.rearrange("p h e -> p (h e)")),
                  r=[("numo", b) for b in range(3)] + TH0,
                  w=[("num", p.g, b, p.cls(fb)) for b in p.nat_blocks(fb)], dma="d_numo%d" % slot)
            return Q

        if p.kind == "A":
            for b in range(3):
                ov, h0, nhb = Obank(b)
                Q.add("dve", lambda e, ov=ov, h0=h0, nhb=nhb: e.tensor_tensor(
                    out=lsum[:, h0:h0 + nhb], in0=ov[:, :, 64], in1=esink[ws][:, h0:h0 + nhb],
                    op=ALU.add), r=[("O", b), ("esink", ws)], w=[("lsum", b)])
            srcs = [Obank(b) for b in range(3)]
            srck = [[("O", b)] for b in range(3)]
        else:
            ni = numin
            NK = [("numin", 0), ("numin", 1), ("numin", 2)]
            Q.add("pool", lambda e: e.tensor_tensor(out=ni[:, 0], in0=ni[:, 0], in1=ni[:, 1], op=ALU.add),
                  r=NK[0:2] + BIGK, w=NK[0:1] + BIGK)
            Q.add("pool", lambda e: e.tensor_tensor(out=ni[:, 0], in0=ni[:, 0], in1=ni[:, 2], op=ALU.add),
                  r=[NK[0], NK[2]] + BIGK, w=NK[0:1] + BIGK)
            Q.add("dve", lambda e: e.tensor_copy(out=lsum[:], in_=ni[:, 0, :, 64]), r=NK[0:1] + BIGK,
                  w=[("lsum", 0), ("lsum", 1), ("lsum", 2)])
            srcs = [(ni[:, 0, 0:6, :], 0, 6), (ni[:, 0, 6:12, :], 6, 6), (ni[:, 0, 12:16, :], 12, 4)]
            srck = [NK[0:1] + BIGK] * 3
        LS = [("lsum", 0), ("lsum", 1), ("lsum", 2)]
        Q.add("dve", lambda e: e.reciprocal(out=rl[:], in_=lsum[:]), r=LS, w=["rl"])
        Q.add("dve", lambda e: e.tensor_scalar(out=fsc[:], in0=rl[:], scalar1=rstdh[h3][:], scalar2=None,
                                               op0=ALU.mult), r=["rl", ("rstdh", h3)], w=["fsc"])
        THK = [("th", bp, 0), ("th", bp, 512)]
        for b in range(3):
            ov, h0, nhb = srcs[b]
            Q.add("dve", lambda e, ov=ov, h0=h0, nhb=nhb: e.tensor_tensor(
                out=th[bp][:, h0:h0 + nhb, :], in0=ov[:, :, 0:64], in1=th[bp][:, h0:h0 + nhb, :],
                op=ALU.mult), r=srck[b] + THK, w=THK)
        Q.add("dve", lambda e: e.tensor_tensor(
            out=yb[:].rearrange("p (h d) -> p h d", d=64), in0=th[bp][:],
            in1=fsc[:].unsqueeze(2).to_broadcast([128, 16, 64]), op=ALU.mult),
            r=THK + ["fsc"], w=["yb"])
        Q.seg("epi1")
        for c in range(8):
            Q.add("pe", lambda e, c=c: e.transpose(out=tp[:, c, :], in_=yb[:, c * 128:(c + 1) * 128],
                                                   identity=ident[:]), r=["yb", "const0"], w=["tp"])
        Q.add("act", lambda e: e.copy(out=yT[:], in_=tp[:]), r=["tp"], w=["yT"])
        for si, c0 in enumerate((0, 512)):
            Q.seg("epi%d" % (2 + si))
            z = zp[si]
            for c in range(8):
                Q.add("pe", lambda e, c=c, z=z, c0=c0: e.matmul(
                    z[:], lhsT=yT[:, c, :], rhs=wout[ws][:, c, c0:c0 + 512], start=(c == 0),
                    stop=(c == 7)), r=["yT"] + wkeys("wout", ws, c0, 512, c), w=[("zp", si)])
            Q.add("dve", lambda e, z=z, c0=c0: e.tensor_tensor(
                out=xs[:, c0:c0 + 512], in0=z[:], in1=xs[:, c0:c0 + 512], op=ALU.add),
                r=[("zp", si), ("xin", slot)], w=[("xin", slot)])
        Q.add("sp", lambda e: e.dma_start(out=out[fb * 128:(fb + 1) * 128, :], in_=xs),
              r=[("xin", slot)], w=[("xd", fb)], dma="d_xout%d" % slot)
        return Q

    def play(seg):
        for a, k in seg:
            P.add(*a, **k)

    ORDER = ["2qk0", "2qk1", "1slab0", "2soft0", "2soft1", "DEFERRED", "2pv0", "2qk2", "2pv1", "2qk3", "1c01",
             "2soft2", "2soft3", "1slab1", "2pv2", "2qk4", "2pv3", "2qk5", "1c02", "1c11", "2soft4", "2soft5",
             "1slab2", "2pv4", "2qk6", "2pv5", "2qk7", "1c03", "1c12", "1c21", "2soft6", "2soft7", "1gate0",
             "2pv6", "2pv7", "1c04", "1c13", "1c22", "3preA", "2epi0", "1gate1", "1qT0", "3preT", "2epi1",
             "2epi2", "2epi3"]
    TAIL = ["c14", "c23", "qT1", "c24", "kT"]
    deferred = []

    def flush_deferred():
        for s in deferred:
            play(s)
        del deferred[:]

    def merge(qs):
        names = set()
        for k, q in qs.items():
            names |= set(k + n for n in q.order if q.segs[n] and not (k == "1" and n in TAIL))
        assert names <= set(ORDER), names - set(ORDER)
        for n in ORDER:
            if n == "DEFERRED":
                flush_deferred()
                continue
            q = qs.get(n[0])
            if q is not None:
                play(q.segs.get(n[1:], []))
        q1 = qs.get("1")
        if q1 is not None:
            for n in TAIL:
                if q1.segs.get(n):
                    deferred.append(q1.segs[n])

    GB = [(pi, fb) for pi in range(len(passes)) for fb in range(NBLK)]
    NG = len(GB)

    def args(gi):
        pi, fb = GB[gi]
        return passes[pi], pi, fb

    def full(q):
        for s in q.lst():
            play(s)

    def s1(gi):
        p1, pi1, fb1 = args(gi)
        return stage1(p1, pi1, fb1, pi1 % 2, gi)

    for st in load_steps(passes[0], 0):
        st()
    if marks is not None:
        marks.append((-1, -1, len(P.ops)))
    for g_ in range(3):
        x_load(*args(g_), g_)
    full(stage_pre(*args(0), 0))
    full(s1(0))
    full(stage_pre(*args(1), 1))
    full(s1(1))
    full(stage_pre(*args(2), 2))
    nxt = []
    held = []
    for gi in range(NG):
        p, pi, fb = args(gi)
        ws = pi % 2
        if fb == 0:
            nxt = load_steps(passes[pi + 1], 1 - ws) if pi + 1 < len(passes) else []
        if fb < len(nxt):
            nxt[fb]()
        if gi + 3 < NG:
            x_load(*args(gi + 3), gi + 3)
        qs = {"2": stage2(p, pi, fb, ws, gi)}
        if gi + 2 < NG:
            if p.kind == "C" and GB[gi + 2][0] != pi:
                held.append(gi + 2)
            else:
                qs["1"] = s1(gi + 2)
        hold_pre = bool(held) and fb == NBLK - 1
        if gi + 3 < NG and not hold_pre:
            qs["3"] = stage_pre(*args(gi + 3), gi + 3)
        merge(qs)
        if held and fb == NBLK - 1:
            flush_deferred()
            for h_ in held:
                full(s1(h_))
            del held[:]
            if gi + 3 < NG:
                full(stage_pre(*args(gi + 3), gi + 3))
        if gi + 1 < NG:
            p1, pi1, fb1 = args(gi + 1)
            num_load(p1, pi1, fb1)
        if marks is not None:
            marks.append((pi, fb, len(P.ops)))
    flush_deferred()

    if sched:
        P.schedule()
    P.emit(final_dma_sems=["d_xout0", "d_xout1", "d_xout2", "d_xout3"], max_ops=max_ops)
    es.close()
    lp.__exit__(None, None, None)
    return nc


def _rope_tables():
    half = 8
    inv = (np.float32(ROPE_THETA) ** (-np.arange(0, 16, 2, dtype=np.float32) / np.float32(16))).astype(np.float32)
    tabs = np.zeros((3, 2, 128, NBLK, half), np.float32)
    p = np.arange(128)
    for oi, (_, dl) in enumerate(B_PAIRS):
        bpc = NBLK // dl
        for fb in range(NBLK):
            r, nb = fb // bpc, fb % bpc
            pos = ((nb * 128 + p) * dl + r).astype(np.float32)
            ang = (pos[:, None] * inv[None, :]).astype(np.float32)
            tabs[oi, 0, :, fb, :] = np.cos(ang.astype(np.float64)).astype(np.float32)
            tabs[oi, 1, :, fb, :] = np.sin(ang.astype(np.float64)).astype(np.float32)
    return tabs.reshape(3, 2, 128, NBLK * half)


def _consts():
    k = np.arange(128)[:, None]
    q = np.arange(128)[None, :]
    c = np.zeros((4, 128, 128), np.float32)
    c[0] = np.eye(128, dtype=np.float32)
    c[1] = (q >= k)
    c[2] = (k > q)
    c[3] = (k >= q)
    return c


def _prep(inputs):
    f = lambda a: np.ascontiguousarray(np.asarray(a, dtype=np.float32))
    rep = lambda v, n: np.repeat(v[:, None, :], n, axis=1)
    shared = {
        "w_in_a": f(inputs["w_in_a"]), "w_out_a": f(inputs["w_out_a"]),
        "w_in_b": f(inputs["w_in_b"]), "w_out_b": f(inputs["w_out_b"]),
        "normw_a": f(np.asarray(inputs["norm_a"]).reshape(2, 8, 128).transpose(0, 2, 1)),
        "normw_b": f(np.asarray(inputs["norm_b"]).reshape(2, 8, 128).transpose(0, 2, 1)),
    }
    qa, ka = np.asarray(inputs["q_gain_a"]), np.asarray(inputs["k_gain_a"])
    ga = np.concatenate([rep(qa, 16), rep(ka, 2)], axis=1).reshape(2, 1, 18 * 64)
    shared["gain_a"] = f(np.broadcast_to(ga, (2, 128, 18 * 64)))
    qb, kb = np.asarray(inputs["q_gain_b"]), np.asarray(inputs["k_gain_b"])
    gb = np.concatenate([np.repeat(qb[:, :, None, :], 16, axis=2), np.repeat(kb[:, :, None, :], 4, axis=2)],
                        axis=2).reshape(2, 3, 1, 20 * 64)
    shared["gain_b"] = f(np.broadcast_to(gb, (2, 3, 128, 20 * 64)))
    shared["sinks"] = f(np.broadcast_to(np.asarray(inputs["sinks_a"])[:, None, :], (2, 128, 16)))
    shared["rope"] = _rope_tables()
    shared["consts"] = _consts()
    x = np.asarray(inputs["x"], dtype=np.float32)
    return [dict(shared, x=np.ascontiguousarray(x[b])) for b in range(x.shape[0])]


_NC_CACHE = {}


def kernel(**inputs):
    in_maps = _prep(inputs)
    if 4 not in _NC_CACHE:
        _NC_CACHE[4] = build(4)
    res = run_bass_kernel_spmd(_NC_CACHE[4], in_maps, core_ids=list(range(8)))
    return np.stack([np.asarray(r["out"], dtype=np.float32) for r in res.results], axis=0)
```

```python
import numpy as np
from contextlib import ExitStack
import concourse.bass as bass
import concourse.mybir as mybir
from concourse.bass_utils import run_bass_kernel_spmd

F32 = mybir.dt.float32
BF16 = mybir.dt.bfloat16
AF = mybir.ActivationFunctionType
ALU = mybir.AluOpType
AX = mybir.AxisListType

S = 4096
D = 1024
NBLK = 32
EPS = 1e-6
ROPE_THETA = 500000.0
B_PAIRS = ((128, 1), (512, 4), (2048, 16))


class _Op:
    __slots__ = ("eng", "fn", "deps", "dma", "signal", "seq", "sem")

    def __init__(self, eng, fn, deps, dma):
        self.eng, self.fn, self.deps, self.dma = eng, fn, deps, dma
        self.signal = False
        self.seq = 0
        self.sem = None


class Prog:
    def __init__(self, nc):
        self.nc = nc
        self.ops = []
        self.last_w = {}
        self.readers = {}

    def add(self, eng, fn, r=(), w=(), dma=None):
        i = len(self.ops)
        deps = set()
        for k in r:
            j = self.last_w.get(k)
            if j is not None:
                deps.add(j)
        for k in w:
            j = self.last_w.get(k)
            if j is not None:
                deps.add(j)
            for x in self.readers.get(k, ()):
                deps.add(x)
        self.ops.append(_Op(eng, fn, deps, dma))
        for k in r:
            self.readers.setdefault(k, []).append(i)
        for k in w:
            self.last_w[k] = i
            self.readers[k] = []
        return i

    LAT_SAME, LAT_X, LAT_DMA = 0.05, 0.3, 2.5
    SCALE = {}
    CP_PRIO = True
    CP_W = 1.0

    class _Fake:
        def __init__(self):
            self.name, self.kw, self.args = None, {}, ()

        def __getattr__(self, name):
            def f(*a, **k):
                self.name, self.args, self.kw = name, a, k
                return self
            return f

    def _cost(self, op):
        rec = Prog._Fake()
        op.fn(rec)
        out = rec.kw.get("out", rec.args[0] if rec.args else None)
        try:
            n = int(out.free_size())
        except Exception:
            n = 256
        e = op.eng
        if op.dma is not None:
            return 0.12
        if e == "pe":
            return 0.05 + n / 1500.0
        if e == "act":
            return 0.12 + n / 1100.0 + (0.1 if "accum_out" in rec.kw else 0.0)
        if e == "dve":
            return 0.08 + n / 1000.0
        if e == "pool":
            if rec.kw.get("op", None) == ALU.pow:
                return 0.1 + 0.2 * n
            return 0.15 + n / 430.0
        return 0.1

    def schedule(self, window=1000000):
        import heapq
        ops = self.ops
        n = len(ops)
        succ = [[] for _ in range(n)]
        indeg = [0] * n
        for i, op in enumerate(ops):
            for d in op.deps:
                succ[d].append(i)
                indeg[i] += 1
        cost = [self._cost(op) * Prog.SCALE.get(op.eng, 1.0) for op in ops]
        tail = [0.0] * n
        if Prog.CP_PRIO:
            for i in range(n - 1, -1, -1):
                t = 0.0
                for s in succ[i]:
                    lat = Prog.LAT_SAME if (ops[s].eng == ops[i].eng and ops[i].dma is None) else Prog.LAT_X
                    if ops[i].dma is not None:
                        lat += Prog.LAT_DMA
                    if tail[s] + lat > t:
                        t = tail[s] + lat
                tail[i] = t + cost[i]
        engs = ["pe", "act", "dve", "pool", "sp"]
        eng_t = {e: 0.0 for e in engs}
        fin = [0.0] * n
        ready_t = [0.0] * n
        wait_h = {e: [] for e in engs}
        avail_h = {e: [] for e in engs}
        for i in range(n):
            if indeg[i] == 0:
                heapq.heappush(wait_h[ops[i].eng], (0.0, i))
        order = []
        done = [False] * n
        lo = 0
        while len(order) < n:
            while lo < n and done[lo]:
                lo += 1
            best = None
            for e in engs:
                wh, ah = wait_h[e], avail_h[e]
                while wh and wh[0][0] <= eng_t[e]:
                    j_ = heapq.heappop(wh)[1]
                    heapq.heappush(ah, (j_ - Prog.CP_W * tail[j_] * 10.0 if Prog.CP_PRIO else j_, j_))
                cand = None
                if ah:
                    cand = (eng_t[e], ah[0][1], 0)
                elif wh:
                    cand = (wh[0][0], wh[0][1], 1)
                if cand is not None and cand[1] < lo + window:
                    if best is None or (cand[0], cand[1]) < (best[0], best[1]):
                        best = cand + (e,)
            if best is None:
                c = [x[1] for e in engs for h in (avail_h[e], wait_h[e]) for x in h]
                i = min(c)
                e = ops[i].eng
                if any(x[1] == i for x in avail_h[e]):
                    avail_h[e] = [x for x in avail_h[e] if x[1] != i]
                    heapq.heapify(avail_h[e])
                else:
                    wait_h[e] = [x for x in wait_h[e] if x[1] != i]
                    heapq.heapify(wait_h[e])
                start = max(eng_t[e], ready_t[i])
            else:
                start, i, src, e = best
                if src == 0:
                    heapq.heappop(avail_h[e])
                else:
                    heapq.heappop(wait_h[e])
            op = ops[i]
            eng_t[e] = start + cost[i]
            fin[i] = start + cost[i] + (Prog.LAT_DMA if op.dma is not None else 0.0)
            done[i] = True
            order.append(i)
            for s in succ[i]:
                lat = Prog.LAT_SAME if (ops[s].eng == e and op.dma is None) else Prog.LAT_X
                ready_t[s] = max(ready_t[s], fin[i] + lat)
                indeg[s] -= 1
                if indeg[s] == 0:
                    heapq.heappush(wait_h[ops[s].eng], (ready_t[s], s))
        pos = {old: new for new, old in enumerate(order)}
        new_ops = [ops[i] for i in order]
        for op in new_ops:
            op.deps = set(pos[d] for d in op.deps)
        self.ops = new_ops
        self.est_us = max(eng_t.values())

    def emit(self, final_dma_sems=(), max_ops=None):
        nc = self.nc
        ops = self.ops if max_ops is None else self.ops[:max_ops]

        def skip(op, dop):
            return op.eng == "pe" and dop.eng == "pe" and dop.dma is None and op.dma is None

        for op in ops:
            for d in op.deps:
                if not skip(op, ops[d]):
                    ops[d].signal = True
        cnt = {}
        dma_names = []
        for op in ops:
            if op.dma is not None:
                if op.dma not in cnt:
                    dma_names.append(op.dma)
                cnt[op.dma] = cnt.get(op.dma, 0) + 16
                op.seq = cnt[op.dma]
                op.sem = op.dma
            elif op.signal:
                cnt[op.eng] = cnt.get(op.eng, 0) + 1
                op.seq = cnt[op.eng]
                op.sem = op.eng
        with ExitStack() as es:
            sems = {}
            for name in ["pe", "act", "dve", "pool"] + dma_names:
                sems[name] = es.enter_context(nc.semaphore("s_" + name))
            block = es.enter_context(nc.Block())

            def run(engname, eng):
                waited = {}
                for op in ops:
                    if op.eng != engname:
                        continue
                    needs = {}
                    for d in op.deps:
                        dop = ops[d]
                        if skip(op, dop):
                            continue
                        if needs.get(dop.sem, 0) < dop.seq:
                            needs[dop.sem] = dop.seq
                    for sname, val in needs.items():
                        if waited.get(sname, 0) < val:
                            eng.wait_ge(sems[sname], val)
                            waited[sname] = val
                    ins = op.fn(eng)
                    if op.dma is not None:
                        ins.then_inc(sems[op.dma], 16)
                    elif op.signal:
                        ins.then_inc(sems[engname], 1)
                if engname == "sp":
                    for name in final_dma_sems:
                        if name in cnt:
                            eng.wait_ge(sems[name], cnt[name])

            @block.tensor
            def _(e):
                run("pe", e)

            @block.scalar
            def _(e):
                run("act", e)

            @block.vector
            def _(e):
                run("dve", e)

            @block.gpsimd
            def _(e):
                run("pool", e)

            @block.sync
            def _(e):
                run("sp", e)


class Pass:
    def __init__(self, kind, layer, idx, g=0):
        self.kind, self.layer, self.idx, self.g = kind, layer, idx, g
        if kind == "A":
            self.delta, self.nkv, self.ncols = 1, 2, 2304
            self.slabs = [(0, 512), (512, 512), (1024, 256), (1280, 512), (1792, 512)]
            self.wname, self.wcol0 = "w_in_a", 0
        elif kind == "G":
            self.delta, self.nkv, self.ncols = B_PAIRS[g][1], 4, 1536
            self.slabs = [(0, 512), (512, 512), (1024, 512)]
            self.wname, self.wcol0 = "w_in_b", 1536 * g
        else:
            self.delta, self.nkv, self.ncols = 1, 0, 1024
            self.slabs = [(0, 512), (512, 512)]
            self.wname, self.wcol0 = "w_in_b", 4608
        self.nh = 16 + self.nkv
        self.has_out = kind in ("A", "C")
        self.bpc = NBLK // self.delta

    def rows(self, t, fb):
        if self.delta == 1:
            return t[fb * 128:(fb + 1) * 128, :]
        r, nb = fb // self.bpc, fb % self.bpc
        return t.rearrange("(j r) e -> r j e", r=self.delta)[r, nb * 128:(nb + 1) * 128, :]

    def nat_blocks(self, fb):
        if self.delta == 1:
            return [fb]
        nb = fb % self.bpc
        return list(range(nb * self.delta, (nb + 1) * self.delta))

    def cls(self, fb):
        return fb // self.bpc


def make_passes(n_layers):
    ps = []
    for layer in range(n_layers):
        idx = layer // 2
        if layer % 2 == 0:
            ps.append(Pass("A", layer, idx))
        else:
            for g in range(3):
                ps.append(Pass("G", layer, idx, g))
            ps.append(Pass("C", layer, idx))
    return ps


class _Rec:
    def __init__(self):
        self.segs = {}
        self.order = []
        self.cur = None
        self.seg("pre")

    def add(self, *a, **k):
        self.segs[self.cur].append((a, k))

    def seg(self, name):
        self.cur = name
        if name not in self.segs:
            self.segs[name] = []
            self.order.append(name)

    def lst(self):
        return [self.segs[n] for n in self.order if self.segs[n]]


def build(n_layers=4, max_ops=None, marks=None, sched=True):
    nc = bass.Bass("TRN2", target_bir_lowering=False)
    dr = {}

    def din(name, shape):
        dr[name] = nc.dram_tensor(name, list(shape), F32, kind="ExternalInput").ap()

    din("x", (S, D))
    din("w_in_a", (2, D, 2304))
    din("w_out_a", (2, D, D))
    din("w_in_b", (2, D, 5632))
    din("w_out_b", (2, D, D))
    din("normw_a", (2, 128, 8))
    din("normw_b", (2, 128, 8))
    din("gain_a", (2, 128, 18 * 64))
    din("gain_b", (2, 3, 128, 20 * 64))
    din("sinks", (2, 128, 16))
    din("rope", (3, 2, 128, NBLK * 8))
    din("consts", (4, 128, 128))
    out = nc.dram_tensor("out", [S, D], F32, kind="ExternalOutput").ap()
    num = [nc.dram_tensor("num%d" % g, [S, 1040], F32, kind="Internal").ap() for g in range(3)]

    P = Prog(nc)
    es = ExitStack()
    lp = nc.allow_low_precision("bf16 matmul operands, fp32 accumulation (problem tolerance)")
    lp.__enter__()

    def sb(name, shape, dt=F32):
        return es.enter_context(nc.sbuf_tensor(name, list(shape), dt))

    def ps(name, shape, dt=F32):
        return es.enter_context(nc.psum_tensor(name, list(shape), dt))

    ident = sb("ident", [128, 128], BF16)
    mask_cur = sb("mask_cur", [128, 128], BF16)
    mask_pa = sb("mask_pa", [128, 128], BF16)
    mask_pb = sb("mask_pb", [128, 128], BF16)
    negh = sb("negh", [128, 32])
    win = [sb("win%d" % i, [128, 8, 2304], BF16) for i in range(2)]
    wout = [sb("wout%d" % i, [128, 8, 1024], BF16) for i in range(2)]
    wstage = [sb("wstage%d" % i, [128, 8, 128]) for i in range(2)]
    normw = [sb("normw%d" % i, [128, 8]) for i in range(2)]
    gain = [sb("gain%d" % i, [128, 20, 64]) for i in range(2)]
    ropec = [sb("ropec%d" % i, [128, NBLK, 8]) for i in range(2)]
    ropes = [sb("ropes%d" % i, [128, NBLK, 8]) for i in range(2)]
    sinkraw = [sb("sinkraw%d" % i, [128, 16]) for i in range(2)]
    esink = [sb("esink%d" % i, [128, 16]) for i in range(2)]
    xin = [sb("xin%d" % i, [128, 1040]) for i in range(4)]
    xb = sb("xb", [128, D], BF16)
    ss = sb("ss", [128, 1])
    rt = sb("rt", [128, 1])
    rstd = [sb("rstd%d" % i, [128, 1]) for i in range(2)]
    rstdh = [sb("rstdh%d" % i, [128, 1]) for i in range(4)]
    xT = [sb("xT%d" % i, [128, 8, 128], BF16) for i in range(2)]
    big = sb("big", [128, 3200])
    zqk = big[:, 0:1280].rearrange("p (h d) -> p h d", d=64)
    sq = big[:, 1280:2560].rearrange("p (h d) -> p h d", d=64)
    numin = big[:, 0:3120].rearrange("p (g h e) -> p g h e", g=3, e=65)
    BIGK = [("zqk", 0), ("zqk", 1), ("zqk", 2), ("sq", 0), ("sq", 1), ("sq", 2)]
    ssh = sb("ssh", [128, 20])
    rht = sb("rht", [128, 20])
    rh = sb("rh", [128, 20])
    ra = sb("ra", [128, 20, 8])
    rb = sb("rb", [128, 20, 8])
    rc = sb("rc", [128, 20, 8])
    rd = sb("rd", [128, 20, 8])
    qbuf = sb("qbuf", [128, 20, 64], BF16)
    kdup = sb("kdup", [128, 4, 2, 64], BF16)
    QT = [sb("QT%d" % i, [128, 8, 128], BF16) for i in range(3)]
    KT2 = [sb("KT2_%d" % i, [128, 4, 128], BF16) for i in range(4)]
    Vp = [sb("Vp%d" % i, [128, 4, 66], BF16) for i in range(4)]
    th = [sb("th%d" % i, [128, 16, 64]) for i in range(3)]
    Pb = [sb("Pb%d" % i, [128, 2, 256], BF16) for i in range(2)]
    lsum = sb("lsum", [128, 16])
    rl = sb("rl", [128, 16])
    fsc = sb("fsc", [128, 16])
    yb = sb("yb", [128, D], BF16)
    yT = sb("yT", [128, 8, 128], BF16)
    tp = ps("tp", [128, 8, 128], BF16)
    zp = [ps("zp%d" % i, [128, 512]) for i in range(2)]
    sp = [ps("sp%d" % i, [128, 2, 256]) for i in range(2)]
    Op = ps("Op", [128, 3, 512])

    def Ohead(h):
        return Op[:, h // 6, (h % 6) * 65:(h % 6) * 65 + 65]

    def Obank(b):
        nhb = 6 if b < 2 else 4
        return Op[:, b, 0:nhb * 65].rearrange("p (h e) -> p h e", e=65), b * 6, nhb

    cstage = xin[3][:, 0:512].rearrange("p (c n) -> p c n", n=128)
    P.add("sp", lambda e: e.dma_start(out=cstage, in_=dr["consts"].rearrange("c p n -> p c n")),
          w=[("xin", 3)], dma="d_const")
    for i, t in enumerate([ident, mask_cur, mask_pa, mask_pb]):
        P.add("dve", lambda e, i=i, t=t: e.tensor_copy(out=t[:], in_=cstage[:, i, :]),
              r=[("xin", 3)], w=["const%d" % i])
    P.add("pool", lambda e: e.memset(negh[:], -0.5), w=["negh"])
    for s_ in range(4):
        P.add("pool", lambda e, s_=s_: e.memset(Vp[s_][:, :, 64:66], 1.0), w=[("Vp1", s_)])

    passes = make_passes(n_layers)

    def wkeys(name, slot, c0, wd, c):
        return [(name, slot, u, c) for u in range(c0 // 128, (c0 + wd) // 128)]

    def load_steps(p, slot):
        steps = []
        L = p.idx
        wsrc = dr[p.wname][L].rearrange("(c q) n -> q c n", q=128)
        nw = dr["normw_a" if p.kind == "A" else "normw_b"][L]

        def small():
            P.add("sp", lambda e: e.dma_start(out=normw[slot][:], in_=nw), w=[("normw", slot)],
                  dma="d_normw%d" % slot)
            if p.kind != "C":
                gsrc = dr["gain_a"][L] if p.kind == "A" else dr["gain_b"][L, p.g]
                P.add("sp", lambda e: e.dma_start(
                    out=gain[slot][:, 0:p.nh, :].rearrange("p h d -> p (h d)"), in_=gsrc),
                    w=[("gain", slot)], dma="d_gain%d" % slot)
                oi = {1: 0, 4: 1, 16: 2}[p.delta]
                P.add("sp", lambda e: e.dma_start(
                    out=ropec[slot][:].rearrange("p b f -> p (b f)"), in_=dr["rope"][oi, 0]),
                    w=[("ropec", slot)], dma="d_ropec%d" % slot)
                P.add("sp", lambda e: e.dma_start(
                    out=ropes[slot][:].rearrange("p b f -> p (b f)"), in_=dr["rope"][oi, 1]),
                    w=[("ropes", slot)], dma="d_ropes%d" % slot)
            if p.kind == "A":
                P.add("sp", lambda e: e.dma_start(out=sinkraw[slot][:], in_=dr["sinks"][L]),
                      w=[("sinkraw", slot)], dma="d_sink%d" % slot)
                P.add("act", lambda e: e.activation(out=esink[slot][:], in_=sinkraw[slot][:],
                                                    func=AF.Exp),
                      r=[("sinkraw", slot)], w=[("esink", slot)])

        steps.append(small)
        slabs = [("in", c0, 128) for c0 in range(0, p.ncols, 128)]
        if p.has_out:
            osrc = dr["w_out_a" if p.kind == "A" else "w_out_b"][L].rearrange(
                "(c q) n -> q c n", q=128)
            slabs += [("out", c0, 128) for c0 in range(0, 1024, 128)]

        def dma_slab(j):
            kind, c0, wd = slabs[j]
            st = j % 2
            if kind == "in":
                P.add("sp", lambda e: e.dma_start(
                    out=wstage[st][:, :, 0:wd], in_=wsrc[:, :, p.wcol0 + c0:p.wcol0 + c0 + wd]),
                    w=[("wstage", st)], dma="d_wst%d" % st)
            else:
                P.add("sp", lambda e: e.dma_start(out=wstage[st][:], in_=osrc[:, :, c0:c0 + 128]),
                      w=[("wstage", st)], dma="d_wst%d" % st)

        def cast_slab(j):
            kind, c0, wd = slabs[j]
            st = j % 2
            for c in range(8):
                if kind == "in":
                    P.add("dve", lambda e, c=c: e.tensor_scalar(
                        out=win[slot][:, c, c0:c0 + wd], in0=wstage[st][:, c, 0:wd],
                        scalar1=normw[slot][:, c:c + 1], scalar2=None, op0=ALU.mult),
                        r=[("wstage", st), ("normw", slot)], w=wkeys("win", slot, c0, wd, c))
                else:
                    P.add("act", lambda e, c=c: e.copy(
                        out=wout[slot][:, c, c0:c0 + 128], in_=wstage[st][:, c, :]),
                        r=[("wstage", st)], w=wkeys("wout", slot, c0, 128, c))

        def mk(j):
            def f():
                if j < len(slabs):
                    dma_slab(j)
                if j >= 1:
                    cast_slab(j - 1)
            return f

        for j in range(len(slabs) + 1):
            steps.append(mk(j))
        return steps

    def x_load(p, pi, fb, gi):
        slot = gi % 4
        src = dr["x"] if pi == 0 else out
        P.add("sp", lambda e: e.dma_start(out=xin[slot][:, 0:D], in_=p.rows(src, fb)),
              r=[("xd", b) for b in p.nat_blocks(fb)], w=[("xin", slot)], dma="d_xin%d" % slot)

    def num_load(p, pi, fb):
        if p.kind == "C":
            for g in range(3):
                dlt = B_PAIRS[g][1]
                P.add("sp", lambda e, g=g: e.dma_start(
                    out=numin[:, g].rearrange("p h e -> p (h e)"),
                    in_=num[g][fb * 128:(fb + 1) * 128, :]),
                    r=[("num", g, fb, r_) for r_ in range(dlt)], w=[("numin", g)] + BIGK,
                    dma="d_numin%d" % g)

    def stage_pre(p, pi, fb, gi):
        Q = _Rec()
        slot = gi % 4
        xs = xin[slot][:, 0:D]
        r2, h3, t2 = gi % 2, gi % 4, gi % 2
        Q.seg("preA")
        Q.add("act", lambda e: e.activation(out=xb[:], in_=xs, func=AF.Square, accum_out=ss[:]),
              r=[("xin", slot)], w=["xb", "ss"])
        Q.add("act", lambda e: e.copy(out=xb[:], in_=xs), r=[("xin", slot)], w=["xb"])
        Q.add("pool", lambda e: e.tensor_scalar(out=rt[:], in0=ss[:], scalar1=1.0 / D, scalar2=EPS,
                                                op0=ALU.mult, op1=ALU.add), r=["ss"], w=["rt"])
        Q.add("pool", lambda e: e.tensor_tensor(out=rstd[r2][:], in0=rt[:], in1=negh[:, 0:1], op=ALU.pow),
              r=["rt", "negh"], w=[("rstd", r2)])
        if p.kind != "G":
            Q.add("pool", lambda e: e.tensor_scalar(out=rstdh[h3][:], in0=rstd[r2][:], scalar1=0.5,
                                                    scalar2=1.0, op0=ALU.mult, op1=ALU.mult),
                  r=[("rstd", r2)], w=[("rstdh", h3)])
        Q.seg("preT")
        for c in range(8):
            Q.add("pe", lambda e, c=c: e.transpose(out=tp[:, c, :], in_=xb[:, c * 128:(c + 1) * 128],
                                                   identity=ident[:]),
                  r=["xb", "const0"], w=["tp"])
        Q.add("act", lambda e: e.copy(out=xT[t2][:], in_=tp[:]), r=["tp"], w=[("xT", t2)])
        return Q

    def stage1(p, pi, fb, ws, gi):
        Q = _Rec()
        bp = gi % 3
        k3 = gi % 4
        r2, h3, xts = gi % 2, gi % 4, gi % 2
        nh, nkv = p.nh, p.nkv

        def slab_mm(si, c0, wd):
            z = zp[si % 2]
            for c in range(8):
                Q.add("pe", lambda e, c=c: e.matmul(z[:, 0:wd], lhsT=xT[xts][:, c, :],
                                                    rhs=win[ws][:, c, c0:c0 + wd],
                                                    start=(c == 0), stop=(c == 7)),
                      r=[("xT", xts)] + wkeys("win", ws, c0, wd, c), w=[("zp", si % 2)])
            return z

        zflat = big[:, 0:1280]
        thf = th[bp][:].rearrange("p h d -> p (h d)")
        if p.kind in ("A", "G"):
            for si in range(2):
                c0, wd = p.slabs[si]
                Q.seg("slab%d" % si)
                z = slab_mm(si, c0, wd)
                Q.add("dve", lambda e, z=z, c0=c0, wd=wd: e.tensor_scalar(
                    out=zflat[:, c0:c0 + wd], in0=z[:, 0:wd], scalar1=rstd[r2][:], scalar2=None,
                    op0=ALU.mult), r=[("zp", si % 2), ("rstd", r2)], w=[("zqk", si)])
            c0, wd = p.slabs[2]
            Q.seg("slab2")
            z = slab_mm(2, c0, wd)
            kw = 64 * nkv
            Q.add("dve", lambda e, z=z: e.tensor_scalar(
                out=zflat[:, 1024:1024 + kw], in0=z[:, 0:kw], scalar1=rstd[r2][:], scalar2=None,
                op0=ALU.mult), r=[("zp", 0), ("rstd", r2)], w=[("zqk", 2)])
            Q.add("dve", lambda e, z=z: e.tensor_scalar(
                out=Vp[k3][:, 0:nkv, 0:64], in0=z[:, kw:2 * kw].rearrange("p (g d) -> p g d", d=64),
                scalar1=rstd[r2][:], scalar2=None, op0=ALU.mult), r=[("zp", 0), ("rstd", r2)],
                w=[("Vp", k3)])
            gslabs = [(3, p.slabs[3]), (4, p.slabs[4])] if p.kind == "A" else []
            gbase = 1280
        else:
            gslabs = [(0, p.slabs[0]), (1, p.slabs[1])]
            gbase = 0
        for gi, (si, (c0, wd)) in enumerate(gslabs):
            Q.seg("gate%d" % gi)
            z = slab_mm(si, c0, wd)
            g0 = c0 - gbase
            Q.add("act", lambda e, z=z, g0=g0, wd=wd: e.activation(
                out=thf[:, g0:g0 + wd], in_=z[:, 0:wd], func=AF.Tanh, scale=rstdh[h3][:]),
                r=[("zp", si % 2), ("rstdh", h3)], w=[("th", bp, g0)])
            Q.add("dve", lambda e, z=z, g0=g0, wd=wd: e.scalar_tensor_tensor(
                out=thf[:, g0:g0 + wd], in0=thf[:, g0:g0 + wd], scalar=1.0, in1=z[:, 0:wd],
                op0=ALU.add, op1=ALU.mult), r=[("zp", si % 2), ("th", bp, g0)], w=[("th", bp, g0)])
        if p.kind == "C":
            return Q

        pieces = [(0, 0, 8), (1, 8, 16), (2, 16, 16 + nkv)]
        for pc, h0, h1 in pieces:
            n = h1 - h0
            ZK = [("zqk", pc)]
            Q.seg("c%d1" % pc)
            Q.add("act", lambda e, h0=h0, h1=h1: e.activation(out=sq[:, h0:h1, :], in_=zqk[:, h0:h1, :],
                                                              func=AF.Square), r=ZK, w=[("sq", pc)])
            Q.add("dve", lambda e, h0=h0, h1=h1: e.tensor_reduce(out=ssh[:, h0:h1], in_=sq[:, h0:h1, :],
                                                                 axis=AX.X, op=ALU.add),
                  r=[("sq", pc)], w=[("ssh", pc)])
            Q.seg("c%d2" % pc)
            Q.add("pool", lambda e, h0=h0, h1=h1: e.tensor_scalar(
                out=rht[:, h0:h1], in0=ssh[:, h0:h1], scalar1=1.0 / 64, scalar2=EPS, op0=ALU.mult,
                op1=ALU.add), r=[("ssh", pc)], w=[("rht", pc)])
            Q.add("pool", lambda e, h0=h0, h1=h1: e.tensor_tensor(
                out=rh[:, h0:h1], in0=rht[:, h0:h1], in1=negh[:, h0:h1], op=ALU.pow),
                r=[("rht", pc), "negh"], w=[("rh", pc)])
            Q.seg("c%d3" % pc)
            Q.add("dve", lambda e, h0=h0, h1=h1, n=n: e.tensor_tensor(
                out=zqk[:, h0:h1, :], in0=zqk[:, h0:h1, :],
                in1=rh[:, h0:h1].unsqueeze(2).to_broadcast([128, n, 64]), op=ALU.mult),
                r=ZK + [("rh", pc)], w=ZK)
            Q.add("dve", lambda e, h0=h0, h1=h1: e.tensor_tensor(
                out=zqk[:, h0:h1, 0:16], in0=zqk[:, h0:h1, 0:16], in1=gain[ws][:, h0:h1, 0:16],
                op=ALU.mult), r=ZK + [("gain", ws)], w=ZK)
            Q.add("dve", lambda e, h0=h0, h1=h1: e.tensor_tensor(
                out=qbuf[:, h0:h1, 16:64], in0=zqk[:, h0:h1, 16:64], in1=gain[ws][:, h0:h1, 16:64],
                op=ALU.mult), r=ZK + [("gain", ws)], w=[("qbuf", pc, 2)])
            Q.seg("c%d4" % pc)
            cb = ropec[ws][:, fb, :].unsqueeze(1).to_broadcast([128, n, 8])
            sbb = ropes[ws][:, fb, :].unsqueeze(1).to_broadcast([128, n, 8])
            t1 = zqk[:, h0:h1, 0:8]
            t2 = zqk[:, h0:h1, 8:16]
            RR = ZK + [("ropec", ws), ("ropes", ws)]
            for nm, tt, a_, b_ in (("ra", ra, t1, cb), ("rb", rb, t2, sbb), ("rc", rc, t2, cb),
                                   ("rd", rd, t1, sbb)):
                Q.add("pool", lambda e, tt=tt, a_=a_, b_=b_, h0=h0, h1=h1: e.tensor_tensor(
                    out=tt[:, h0:h1, :], in0=a_, in1=b_, op=ALU.mult), r=RR, w=[(nm, pc)])
            Q.add("pool", lambda e, h0=h0, h1=h1: e.tensor_tensor(
                out=qbuf[:, h0:h1, 0:8], in0=ra[:, h0:h1, :], in1=rb[:, h0:h1, :], op=ALU.subtract),
                r=[("ra", pc), ("rb", pc)], w=[("qbuf", pc, 0)])
            Q.add("pool", lambda e, h0=h0, h1=h1: e.tensor_tensor(
                out=qbuf[:, h0:h1, 8:16], in0=rc[:, h0:h1, :], in1=rd[:, h0:h1, :], op=ALU.add),
                r=[("rc", pc), ("rd", pc)], w=[("qbuf", pc, 1)])
            QB = [("qbuf", pc, 0), ("qbuf", pc, 1), ("qbuf", pc, 2)]
            if pc < 2:
                Q.seg("qT%d" % pc)
                for c in range(4):
                    cc = 4 * pc + c
                    Q.add("pe", lambda e, cc=cc: e.transpose(
                        out=tp[:, cc, :], in_=qbuf[:, 2 * cc:2 * cc + 2, :].rearrange("p h d -> p (h d)"),
                        identity=ident[:]), r=QB + ["const0"], w=["tp"])
                Q.add("act", lambda e, pc=pc: e.copy(out=QT[bp][:, 4 * pc:4 * pc + 4, :],
                                                     in_=tp[:, 4 * pc:4 * pc + 4, :]),
                      r=["tp"], w=[("QT", bp, pc)])
            else:
                Q.add("dve", lambda e: e.tensor_copy(
                    out=kdup[:, 0:nkv, :, :],
                    in_=qbuf[:, 16:16 + nkv, :].unsqueeze(2).to_broadcast([128, nkv, 2, 64])),
                    r=QB, w=["kdup"])
                Q.seg("kT")
                for g in range(nkv):
                    Q.add("pe", lambda e, g=g: e.transpose(
                        out=tp[:, g, :], in_=kdup[:, g, :, :].rearrange("p t d -> p (t d)"),
                        identity=ident[:]), r=["kdup", "const0"], w=["tp"])
                Q.add("act", lambda e: e.copy(out=KT2[k3][:, 0:nkv, :], in_=tp[:, 0:nkv, :]),
                      r=["tp"], w=[("KT2", k3)])
        return Q

    def stage2(p, pi, fb, ws, gi):
        Q = _Rec()
        slot = gi % 4
        bp = gi % 3
        h3 = gi % 4
        kv, kvp = gi % 4, (gi - 1) % 4
        numo = xin[slot][:, 0:1040].rearrange("p (h e) -> p h e", e=65)
        nb = fb % p.bpc
        has_prev = nb > 0
        nkv = p.nkv
        xs = xin[slot][:, 0:D]
        if p.kind in ("A", "G"):
            rep2 = (16 // nkv) // 2
            mprev = mask_pa if p.kind == "A" else mask_pb
            bi = 0
            for g in range(nkv):
                for par in range(2):
                    for cc in range(0, rep2, 2):
                        c0 = g * rep2 + cc
                        sl = bi % 2
                        bi += 1
                        pr = slice(par * 64, par * 64 + 64)
                        Q.seg("qk%d" % (bi - 1))
                        Q.add("pe", lambda e, sl=sl, pr=pr, g=g, c0=c0: e.matmul(
                            sp[sl][:, 1, :], lhsT=KT2[kv][pr, g, :], rhs=QT[bp][pr, c0:c0 + 2, :],
                            start=True, stop=True), r=[("KT2", kv), ("QT", bp, c0 // 4)], w=[("sp", sl)])
                        if has_prev:
                            Q.add("pe", lambda e, sl=sl, pr=pr, g=g, c0=c0: e.matmul(
                                sp[sl][:, 0, :], lhsT=KT2[kvp][pr, g, :], rhs=QT[bp][pr, c0:c0 + 2, :],
                                start=True, stop=True), r=[("KT2", kvp), ("QT", bp, c0 // 4)],
                                w=[("sp", sl)])
                            Q.seg("soft%d" % (bi - 1))
                            Q.add("act", lambda e, sl=sl: e.activation(
                                out=Pb[sl][:], in_=sp[sl][:], func=AF.Exp, scale=0.125),
                                r=[("sp", sl), ("sp", sl)], w=[("Pb", sl, 0), ("Pb", sl, 1)])
                            Q.add("dve", lambda e, sl=sl: e.tensor_tensor(
                                out=Pb[sl][:, 0, :].rearrange("p (h q) -> p h q", q=128),
                                in0=Pb[sl][:, 0, :].rearrange("p (h q) -> p h q", q=128),
                                in1=mprev[:].unsqueeze(1).to_broadcast([128, 2, 128]), op=ALU.mult),
                                r=[("Pb", sl, 0), "const2", "const3"], w=[("Pb", sl, 0)])
                        else:
                            Q.seg("soft%d" % (bi - 1))
                            Q.add("act", lambda e, sl=sl: e.activation(
                                out=Pb[sl][:, 1, :], in_=sp[sl][:, 1, :], func=AF.Exp, scale=0.125),
                                r=[("sp", sl)], w=[("Pb", sl, 1)])
                        Q.add("dve", lambda e, sl=sl: e.tensor_tensor(
                            out=Pb[sl][:, 1, :].rearrange("p (h q) -> p h q", q=128),
                            in0=Pb[sl][:, 1, :].rearrange("p (h q) -> p h q", q=128),
                            in1=mask_cur[:].unsqueeze(1).to_broadcast([128, 2, 128]), op=ALU.mult),
                            r=[("Pb", sl, 1), "const1"], w=[("Pb", sl, 1)])
                        Q.seg("pv%d" % (bi - 1))
                        for i in range(2):
                            h = 2 * (c0 + i) + par
                            if has_prev:
                                Q.add("pe", lambda e, sl=sl, i=i, h=h, g=g: e.matmul(
                                    Ohead(h), lhsT=Pb[sl][:, 0, i * 128:(i + 1) * 128],
                                    rhs=Vp[kvp][:, g, 0:65], start=True, stop=False),
                                    r=[("Pb", sl, 0), ("Vp", kvp), ("Vp1", kvp)], w=[("O", h // 6)])
                            Q.add("pe", lambda e, sl=sl, i=i, h=h, g=g: e.matmul(
                                Ohead(h), lhsT=Pb[sl][:, 1, i * 128:(i + 1) * 128],
                                rhs=Vp[kv][:, g, 0:65], start=(not has_prev), stop=True),
                                r=[("Pb", sl, 1), ("Vp", kv), ("Vp1", kv)], w=[("O", h // 6)])

        Q.seg("epi0")
        if p.kind == "G":
            TH0 = [("xin", slot)]
            for b in range(3):
                ov, h0, nhb = Obank(b)
                if b == 1:
                    Q.add("dve", lambda e, ov=ov, h0=h0, nhb=nhb: e.tensor_copy(
                        out=numo[:, h0:h0 + nhb, :], in_=ov), r=[("O", b)], w=[("numo", b)] + TH0)
                else:
                    Q.add("act", lambda e, ov=ov, h0=h0, nhb=nhb: e.copy(
                        out=numo[:, h0:h0 + nhb, :], in_=ov), r=[("O", b)], w=[("numo", b)] + TH0)
            Q.add("sp", lambda e: e.dma_start(out=p.rows(num[p.g], fb),
                                              in_=numo.rearrange("p h e -> p (h e)")),
                  r=[("numo", b) for b in range(3)] + TH0,
                  w=[("num", p.g, b, p.cls(fb)) for b in p.nat_blocks(fb)], dma="d_numo%d" % slot)
            return Q

        if p.kind == "A":
            for b in range(3):
                ov, h0, nhb = Obank(b)
                Q.add("dve", lambda e, ov=ov, h0=h0, nhb=nhb: e.tensor_tensor(
                    out=lsum[:, h0:h0 + nhb], in0=ov[:, :, 64], in1=esink[ws][:, h0:h0 + nhb],
                    op=ALU.add), r=[("O", b), ("esink", ws)], w=[("lsum", b)])
            srcs = [Obank(b) for b in range(3)]
            srck = [[("O", b)] for b in range(3)]
        else:
            ni = numin
            NK = [("numin", 0), ("numin", 1), ("numin", 2)]
            Q.add("pool", lambda e: e.tensor_tensor(out=ni[:, 0], in0=ni[:, 0], in1=ni[:, 1], op=ALU.add),
                  r=NK[0:2] + BIGK, w=NK[0:1] + BIGK)
            Q.add("pool", lambda e: e.tensor_tensor(out=ni[:, 0], in0=ni[:, 0], in1=ni[:, 2], op=ALU.add),
                  r=[NK[0], NK[2]] + BIGK, w=NK[0:1] + BIGK)
            Q.add("dve", lambda e: e.tensor_copy(out=lsum[:], in_=ni[:, 0, :, 64]), r=NK[0:1] + BIGK,
                  w=[("lsum", 0), ("lsum", 1), ("lsum", 2)])
            srcs = [(ni[:, 0, 0:6, :], 0, 6), (ni[:, 0, 6:12, :], 6, 6), (ni[:, 0, 12:16, :], 12, 4)]
            srck = [NK[0:1] + BIGK] * 3
        LS = [("lsum", 0), ("lsum", 1), ("lsum", 2)]
        Q.add("dve", lambda e: e.reciprocal(out=rl[:], in_=lsum[:]), r=LS, w=["rl"])
        Q.add("dve", lambda e: e.tensor_scalar(out=fsc[:], in0=rl[:], scalar1=rstdh[h3][:], scalar2=None,
                                               op0=ALU.mult), r=["rl", ("rstdh", h3)], w=["fsc"])
        THK = [("th", bp, 0), ("th", bp, 512)]
        for b in range(3):
            ov, h0, nhb = srcs[b]
            Q.add("dve", lambda e, ov=ov, h0=h0, nhb=nhb: e.tensor_tensor(
                out=th[bp][:, h0:h0 + nhb, :], in0=ov[:, :, 0:64], in1=th[bp][:, h0:h0 + nhb, :],
                op=ALU.mult), r=srck[b] + THK, w=THK)
        Q.add("dve", lambda e: e.tensor_tensor(
            out=yb[:].rearrange("p (h d) -> p h d", d=64), in0=th[bp][:],
            in1=fsc[:].unsqueeze(2).to_broadcast([128, 16, 64]), op=ALU.mult),
            r=THK + ["fsc"], w=["yb"])
        Q.seg("epi1")
        for c in range(8):
            Q.add("pe", lambda e, c=c: e.transpose(out=tp[:, c, :], in_=yb[:, c * 128:(c + 1) * 128],
                                                   identity=ident[:]), r=["yb", "const0"], w=["tp"])
        Q.add("act", lambda e: e.copy(out=yT[:], in_=tp[:]), r=["tp"], w=["yT"])
        for si, c0 in enumerate((0, 512)):
            Q.seg("epi%d" % (2 + si))
            z = zp[si]
            for c in range(8):
                Q.add("pe", lambda e, c=c, z=z, c0=c0: e.matmul(
                    z[:], lhsT=yT[:, c, :], rhs=wout[ws][:, c, c0:c0 + 512], start=(c == 0),
                    stop=(c == 7)), r=["yT"] + wkeys("wout", ws, c0, 512, c), w=[("zp", si)])
            Q.add("dve", lambda e, z=z, c0=c0: e.tensor_tensor(
                out=xs[:, c0:c0 + 512], in0=z[:], in1=xs[:, c0:c0 + 512], op=ALU.add),
                r=[("zp", si), ("xin", slot)], w=[("xin", slot)])
        Q.add("sp", lambda e: e.dma_start(out=out[fb * 128:(fb + 1) * 128, :], in_=xs),
              r=[("xin", slot)], w=[("xd", fb)], dma="d_xout%d" % slot)
        return Q

    def play(seg):
        for a, k in seg:
            P.add(*a, **k)

    ORDER = ["2qk0", "2qk1", "1slab0", "2soft0", "2soft1", "DEFERRED", "2pv0", "2qk2", "2pv1", "2qk3", "1c01",
             "2soft2", "2soft3", "1slab1", "2pv2", "2qk4", "2pv3", "2qk5", "1c02", "1c11", "2soft4", "2soft5",
             "1slab2", "2pv4", "2qk6", "2pv5", "2qk7", "1c03", "1c12", "1c21", "2soft6", "2soft7", "1gate0",
             "2pv6", "2pv7", "1c04", "1c13", "1c22", "3preA", "2epi0", "1gate1", "1qT0", "3preT", "2epi1",
             "2epi2", "2epi3"]
    TAIL = ["c14", "c23", "qT1", "c24", "kT"]
    deferred = []

    def flush_deferred():
        for s in deferred:
            play(s)
        del deferred[:]

    def merge(qs):
        names = set()
        for k, q in qs.items():
            names |= set(k + n for n in q.order if q.segs[n] and not (k == "1" and n in TAIL))
        assert names <= set(ORDER), names - set(ORDER)
        for n in ORDER:
            if n == "DEFERRED":
                flush_deferred()
                continue
            q = qs.get(n[0])
            if q is not None:
                play(q.segs.get(n[1:], []))
        q1 = qs.get("1")
        if q1 is not None:
            for n in TAIL:
                if q1.segs.get(n):
                    deferred.append(q1.segs[n])

    GB = [(pi, fb) for pi in range(len(passes)) for fb in range(NBLK)]
    NG = len(GB)

    def args(gi):
        pi, fb = GB[gi]
        return passes[pi], pi, fb

    def full(q):
        for s in q.lst():
            play(s)

    def s1(gi):
        p1, pi1, fb1 = args(gi)
        return stage1(p1, pi1, fb1, pi1 % 2, gi)

    for st in load_steps(passes[0], 0):
        st()
    if marks is not None:
        marks.append((-1, -1, len(P.ops)))
    for g_ in range(3):
        x_load(*args(g_), g_)
    full(stage_pre(*args(0), 0))
    full(s1(0))
    full(stage_pre(*args(1), 1))
    full(s1(1))
    full(stage_pre(*args(2), 2))
    nxt = []
    held = []
    for gi in range(NG):
        p, pi, fb = args(gi)
        ws = pi % 2
        if fb == 0:
            nxt = load_steps(passes[pi + 1], 1 - ws) if pi + 1 < len(passes) else []
        if fb < len(nxt):
            nxt[fb]()
        if gi + 3 < NG:
            x_load(*args(gi + 3), gi + 3)
        qs = {"2": stage2(p, pi, fb, ws, gi)}
        if gi + 2 < NG:
            if p.kind == "C" and GB[gi + 2][0] != pi:
                held.append(gi + 2)
            else:
                qs["1"] = s1(gi + 2)
        hold_pre = bool(held) and fb == NBLK - 1
        if gi + 3 < NG and not hold_pre:
            qs["3"] = stage_pre(*args(gi + 3), gi + 3)
        merge(qs)
        if held and fb == NBLK - 1:
            flush_deferred()
            for h_ in held:
                full(s1(h_))
            del held[:]
            if gi + 3 < NG:
                full(stage_pre(*args(gi + 3), gi + 3))
        if gi + 1 < NG:
            p1, pi1, fb1 = args(gi + 1)
            num_load(p1, pi1, fb1)
        if marks is not None:
            marks.append((pi, fb, len(P.ops)))
    flush_deferred()

    if sched:
        P.schedule()
    P.emit(final_dma_sems=["d_xout0", "d_xout1", "d_xout2", "d_xout3"], max_ops=max_ops)
    es.close()
    lp.__exit__(None, None, None)
    return nc


def _rope_tables():
    half = 8
    inv = (np.float32(ROPE_THETA) ** (-np.arange(0, 16, 2, dtype=np.float32) / np.float32(16))).astype(np.float32)
    tabs = np.zeros((3, 2, 128, NBLK, half), np.float32)
    p = np.arange(128)
    for oi, (_, dl) in enumerate(B_PAIRS):
        bpc = NBLK // dl
        for fb in range(NBLK):
            r, nb = fb // bpc, fb % bpc
            pos = ((nb * 128 + p) * dl + r).astype(np.float32)
            ang = (pos[:, None] * inv[None, :]).astype(np.float32)
            tabs[oi, 0, :, fb, :] = np.cos(ang.astype(np.float64)).astype(np.float32)
            tabs[oi, 1, :, fb, :] = np.sin(ang.astype(np.float64)).astype(np.float32)
    return tabs.reshape(3, 2, 128, NBLK * half)


def _consts():
    k = np.arange(128)[:, None]
    q = np.arange(128)[None, :]
    c = np.zeros((4, 128, 128), np.float32)
    c[0] = np.eye(128, dtype=np.float32)
    c[1] = (q >= k)
    c[2] = (k > q)
    c[3] = (k >= q)
    return c


def _prep(inputs):
    f = lambda a: np.ascontiguousarray(np.asarray(a, dtype=np.float32))
    rep = lambda v, n: np.repeat(v[:, None, :], n, axis=1)
    shared = {
        "w_in_a": f(inputs["w_in_a"]), "w_out_a": f(inputs["w_out_a"]),
        "w_in_b": f(inputs["w_in_b"]), "w_out_b": f(inputs["w_out_b"]),
        "normw_a": f(np.asarray(inputs["norm_a"]).reshape(2, 8, 128).transpose(0, 2, 1)),
        "normw_b": f(np.asarray(inputs["norm_b"]).reshape(2, 8, 128).transpose(0, 2, 1)),
    }
    qa, ka = np.asarray(inputs["q_gain_a"]), np.asarray(inputs["k_gain_a"])
    ga = np.concatenate([rep(qa, 16), rep(ka, 2)], axis=1).reshape(2, 1, 18 * 64)
    shared["gain_a"] = f(np.broadcast_to(ga, (2, 128, 18 * 64)))
    qb, kb = np.asarray(inputs["q_gain_b"]), np.asarray(inputs["k_gain_b"])
    gb = np.concatenate([np.repeat(qb[:, :, None, :], 16, axis=2), np.repeat(kb[:, :, None, :], 4, axis=2)],
                        axis=2).reshape(2, 3, 1, 20 * 64)
    shared["gain_b"] = f(np.broadcast_to(gb, (2, 3, 128, 20 * 64)))
    shared["sinks"] = f(np.broadcast_to(np.asarray(inputs["sinks_a"])[:, None, :], (2, 128, 16)))
    shared["rope"] = _rope_tables()
    shared["consts"] = _consts()
    x = np.asarray(inputs["x"], dtype=np.float32)
    return [dict(shared, x=np.ascontiguousarray(x[b])) for b in range(x.shape[0])]


_NC_CACHE = {}


def kernel(**inputs):
    in_maps = _prep(inputs)
    if 4 not in _NC_CACHE:
        _NC_CACHE[4] = build(4)
    res = run_bass_kernel_spmd(_NC_CACHE[4], in_maps, core_ids=list(range(8)))
    return np.stack([np.asarray(r["out"], dtype=np.float32) for r in res.results], axis=0)
```
